# Optimizing a Trainium2 kernel written in Bass

```python
import math
import jax, jax.numpy as jnp
from jax import lax
import numpy as np

D_MODEL = 2048
BATCH = 8
SEQ = 4096
DEPTH = 2
DEC_BATCH = 16
DEC_SEQ = 32
PAST_LEN = 2048

CHUNK = 64
N_HEADS = 8
HEAD_DIM = 128
N_SUB = 2 * N_HEADS
D_ATT = N_HEADS * 2 * HEAD_DIM
Q_BLOCK = 128
D_SSM = D_MODEL
GROUP_CH = 16
N_GROUPS = D_SSM // GROUP_CH
STATE_P = 64
SCAN_BLOCK = CHUNK
D_FF = 5632
CONV_W = 3
LN_EPS = 1e-5
ALPHA = (2 * DEPTH) ** 0.25
BETA = (8 * DEPTH) ** -0.25
N_SSM_LAYERS = (DEPTH + 1) // 2
N_ATTN_LAYERS = DEPTH // 2

kernel_name = "hybrid_s5_diffattn_convffn_stream_step"


def layer_norm(x, g, b):
    xf = x.astype(jnp.float32)
    mu = jnp.mean(xf, axis=-1, keepdims=True)
    var = jnp.mean(jnp.square(xf - mu), axis=-1, keepdims=True)
    return ((xf - mu) * lax.rsqrt(var + LN_EPS) * g.astype(jnp.float32) + b.astype(jnp.float32)).astype(x.dtype)


def ada_mod(c, w, b):
    m = (jax.nn.silu(c) @ w + b)[:, None, :]
    return jnp.split(m, 6, axis=-1)


def conv_ffn(h, buf, w_up, w_conv, b_conv, w_down):
    g, v = jnp.split(h @ w_up, 2, axis=-1)
    gp = jnp.concatenate([buf.astype(g.dtype), g], axis=1)
    L = g.shape[1]
    conv = b_conv
    for kk in range(CONV_W):
        conv = conv + gp[:, kk:kk + L] * w_conv[kk]
    out = (jax.nn.gelu(conv, approximate=False) * v) @ w_down
    return out, gp[:, -(CONV_W - 1):]


def ssm_discretise(lam_re, lam_im, log_step, b_re, b_im):
    lr = jnp.minimum(lam_re.astype(jnp.float32), -1e-4)
    li = lam_im.astype(jnp.float32)
    dt = jnp.exp(log_step.astype(jnp.float32))[:, None]
    mag = jnp.exp(lr * dt)
    abar_re = mag * jnp.cos(li * dt)
    abar_im = mag * jnp.sin(li * dt)
    nr = abar_re - 1.0
    ni = abar_im
    den = lr * lr + li * li
    kr = (nr * lr + ni * li) / den
    ki = (ni * lr - nr * li) / den
    br = b_re.astype(jnp.float32)
    bi = b_im.astype(jnp.float32)
    bbar_re = kr[..., None] * br - ki[..., None] * bi
    bbar_im = kr[..., None] * bi + ki[..., None] * br
    return abar_re, abar_im, bbar_re, bbar_im


def _cmul_combine(e1, e2):
    a1r, a1i, b1r, b1i = e1
    a2r, a2i, b2r, b2i = e2
    return (a2r * a1r - a2i * a1i,
            a2r * a1i + a2i * a1r,
            a2r * b1r - a2i * b1i + b2r,
            a2r * b1i + a2i * b1r + b2i)


def ssm_scan(u, s_re, s_im, abar_re, abar_im, bbar_re, bbar_im, c_re, c_im):
    bsz, L = u.shape[0], u.shape[1]
    blk = L if L <= SCAN_BLOCK else SCAN_BLOCK
    nblk = L // blk
    ub = u.reshape(bsz, nblk, blk, N_GROUPS, GROUP_CH).swapaxes(0, 1)
    cr = c_re.astype(jnp.float32)
    ci = c_im.astype(jnp.float32)

    def step(carry, ublk):
        sr, si = carry
        br = jnp.einsum('btgc,gpc->btgp', ublk, bbar_re)
        bi = jnp.einsum('btgc,gpc->btgp', ublk, bbar_im)
        br = br.at[:, 0].add(abar_re * sr - abar_im * si)
        bi = bi.at[:, 0].add(abar_re * si + abar_im * sr)
        ar = jnp.broadcast_to(abar_re, br.shape)
        ai = jnp.broadcast_to(abar_im, bi.shape)
        _, _, hr, hi = lax.associative_scan(_cmul_combine, (ar, ai, br, bi), axis=1)
        y = jnp.einsum('btgp,gcp->btgc', hr, cr) - jnp.einsum('btgp,gcp->btgc', hi, ci)
        return (hr[:, -1], hi[:, -1]), y

    (sr, si), yb = lax.scan(step, (s_re.astype(jnp.float32), s_im.astype(jnp.float32)), ub)
    y = yb.swapaxes(0, 1).reshape(bsz, L, N_GROUPS, GROUP_CH)
    return y, sr, si


def ssm_mixer(h, s_re, s_im, w_in, disc, c_re, c_im, d_skip, w_glu):
    bsz, L, _ = h.shape
    u = (h @ w_in).astype(jnp.float32).reshape(bsz, L, N_GROUPS, GROUP_CH)
    y, sr, si = ssm_scan(u, s_re, s_im, *disc, c_re, c_im)
    y = y + d_skip.astype(jnp.float32).reshape(N_GROUPS, GROUP_CH) * u
    z = jax.nn.gelu(y.reshape(bsz, L, D_SSM), approximate=False).astype(h.dtype)
    a, g = jnp.split(z @ w_glu, 2, axis=-1)
    return a * jax.nn.sigmoid(g), sr, si


def diff_lambda(lq1, lk1, lq2, lk2, lam_init):
    f = jnp.float32
    return (jnp.exp(jnp.sum(lq1.astype(f) * lk1.astype(f)))
            - jnp.exp(jnp.sum(lq2.astype(f) * lk2.astype(f))) + lam_init)


def qkv_proj(h, w_qkv):
    bsz, L, _ = h.shape
    q, k, v = jnp.split(h @ w_qkv, 3, axis=-1)
    return (q.reshape(bsz, L, N_SUB, HEAD_DIM), k.reshape(bsz, L, N_SUB, HEAD_DIM),
            v.reshape(bsz, L, N_HEADS, 2 * HEAD_DIM))


def diff_attend(q, k, v, qpos, kpos, lam):
    s = jnp.einsum('bqhd,bkhd->bhqk', q, k, preferred_element_type=jnp.float32) * (HEAD_DIM ** -0.5)
    mask = (kpos[None, :] // CHUNK) <= (qpos[:, None] // CHUNK)
    s = jnp.where(mask, s, -1e30)
    p = jax.nn.softmax(s, axis=-1)
    bsz, _, tq, tk = p.shape
    p = p.reshape(bsz, N_HEADS, 2, tq, tk)
    a = p[:, :, 0] - lam * p[:, :, 1]
    return jnp.einsum('bhqk,bkhe->bqhe', a.astype(v.dtype), v)


def diff_out(o, subln_g, lam_init, w_o):
    of = o.astype(jnp.float32)
    of = of * lax.rsqrt(jnp.mean(jnp.square(of), axis=-1, keepdims=True) + LN_EPS)
    of = of * subln_g.astype(jnp.float32) * (1.0 - lam_init)
    bsz, L = o.shape[0], o.shape[1]
    return of.reshape(bsz, L, D_ATT).astype(o.dtype) @ w_o


def diff_attn_prompt(h, w_qkv, lam, lam_init, subln_g, w_o):
    bsz, L, _ = h.shape
    q, k, v = qkv_proj(h, w_qkv)
    nq = L // Q_BLOCK
    qb = q.reshape(bsz, nq, Q_BLOCK, N_SUB, HEAD_DIM).swapaxes(0, 1)
    kpos = jnp.arange(L)

    def blk(args):
        qblk, bi = args
        qpos = bi * Q_BLOCK + jnp.arange(Q_BLOCK)
        return diff_attend(qblk, k, v, qpos, kpos, lam)

    o = lax.map(blk, (qb, jnp.arange(nq)))
    o = o.swapaxes(0, 1).reshape(bsz, L, N_HEADS, 2 * HEAD_DIM)
    return diff_out(o, subln_g, lam_init, w_o), k, v


def diff_attn_sample(h, ck, cv, w_qkv, lam, lam_init, subln_g, w_o):
    T = h.shape[1]
    P = ck.shape[1]
    q, k, v = qkv_proj(h, w_qkv)
    kk = jnp.concatenate([ck.astype(k.dtype), k], axis=1)
    vv = jnp.concatenate([cv.astype(v.dtype), v], axis=1)
    qpos = P + jnp.arange(T)
    kpos = jnp.arange(P + T)
    o = diff_attend(q, kk, vv, qpos, kpos, lam)
    return diff_out(o, subln_g, lam_init, w_o), k, v


def setup_inputs(seed: int = 0) -> dict:
    key = jax.random.key(seed)
    ks = iter(jax.random.split(key, 48))
    f = jnp.float32

    def nrm(shape, scale):
        return jax.random.normal(next(ks), shape, f) * scale

    NS, NA = N_SSM_LAYERS, N_ATTN_LAYERS
    d = {}
    d['x_prompt'] = nrm((BATCH, SEQ, D_MODEL), 1.0)
    d['x_sample'] = nrm((DEC_BATCH, DEC_SEQ, D_MODEL), 1.0)
    d['c_prompt'] = nrm((BATCH, D_MODEL), 1.0)
    d['c_sample'] = nrm((DEC_BATCH, D_MODEL), 1.0)
    d['cache_k'] = nrm((NA, DEC_BATCH, PAST_LEN, N_SUB, HEAD_DIM), 1.0)
    d['cache_v'] = nrm((NA, DEC_BATCH, PAST_LEN, N_HEADS, 2 * HEAD_DIM), BETA)
    d['state_ssm_re'] = nrm((NS, DEC_BATCH, N_GROUPS, STATE_P), 0.5)
    d['state_ssm_im'] = nrm((NS, DEC_BATCH, N_GROUPS, STATE_P), 0.5)
    d['state_conv'] = nrm((DEPTH, DEC_BATCH, CONV_W - 1, D_FF), BETA)
    d['w_ada'] = nrm((DEPTH, D_MODEL, 6 * D_MODEL), 0.1 * D_MODEL ** -0.5)
    d['b_ada'] = nrm((DEPTH, 6 * D_MODEL), 0.01)
    d['ln_g'] = 1.0 + nrm((DEPTH, 2, D_MODEL), 0.02)
    d['ln_b'] = nrm((DEPTH, 2, D_MODEL), 0.02)
    d['w_up'] = nrm((DEPTH, D_MODEL, 2 * D_FF), BETA * D_MODEL ** -0.5)
    d['w_dconv'] = nrm((DEPTH, CONV_W, D_FF), CONV_W ** -0.5)
    d['b_dconv'] = nrm((DEPTH, D_FF), 0.02)
    d['w_down'] = nrm((DEPTH, D_FF, D_MODEL), BETA * D_FF ** -0.5)
    d['w_ssm_in'] = nrm((NS, D_MODEL, D_SSM), BETA * D_MODEL ** -0.5)
    d['ssm_lam_re'] = -0.5 + nrm((NS, N_GROUPS, STATE_P), 0.01)
    d['ssm_lam_im'] = math.pi * jnp.arange(STATE_P, dtype=f)[None, None, :] + nrm((NS, N_GROUPS, STATE_P), 0.01)
    d['ssm_log_step'] = jax.random.uniform(next(ks), (NS, N_GROUPS), f, math.log(1e-3), math.log(1e-1))
    d['ssm_b_re'] = nrm((NS, N_GROUPS, STATE_P, GROUP_CH), (2 * GROUP_CH) ** -0.5)
    d['ssm_b_im'] = nrm((NS, N_GROUPS, STATE_P, GROUP_CH), (2 * GROUP_CH) ** -0.5)
    d['ssm_c_re'] = nrm((NS, N_GROUPS, GROUP_CH, STATE_P), (2 * STATE_P) ** -0.5)
    d['ssm_c_im'] = nrm((NS, N_GROUPS, GROUP_CH, STATE_P), (2 * STATE_P) ** -0.5)
    d['ssm_d'] = nrm((NS, D_SSM), 1.0)
    d['w_glu'] = nrm((NS, D_SSM, 2 * D_MODEL), BETA * D_SSM ** -0.5)
    wq = nrm((NA, D_MODEL, D_ATT), D_MODEL ** -0.5)
    wk = nrm((NA, D_MODEL, D_ATT), D_MODEL ** -0.5)
    wv = nrm((NA, D_MODEL, D_ATT), BETA * D_MODEL ** -0.5)
    d['w_qkv'] = jnp.concatenate([wq, wk, wv], axis=-1)
    d['lam_q1'] = nrm((NA, HEAD_DIM), 0.1)
    d['lam_k1'] = nrm((NA, HEAD_DIM), 0.1)
    d['lam_q2'] = nrm((NA, HEAD_DIM), 0.1)
    d['lam_k2'] = nrm((NA, HEAD_DIM), 0.1)
    d['subln_g'] = 1.0 + nrm((NA, 2 * HEAD_DIM), 0.02)
    d['w_o'] = nrm((NA, D_ATT, D_MODEL), BETA * D_ATT ** -0.5)
    return d


def reference(x_prompt, x_sample, c_prompt, c_sample, cache_k, cache_v, state_ssm_re, state_ssm_im,
              state_conv, w_ada, b_ada, ln_g, ln_b, w_up, w_dconv, b_dconv, w_down, w_ssm_in,
              ssm_lam_re, ssm_lam_im, ssm_log_step, ssm_b_re, ssm_b_im, ssm_c_re, ssm_c_im, ssm_d,
              w_glu, w_qkv, lam_q1, lam_k1, lam_q2, lam_k2, subln_g, w_o):
    xp, xs = x_prompt, x_sample
    bp, bs = xp.shape[0], xs.shape[0]
    kp_l, vp_l, ks_l, vs_l = [], [], [], []
    srp_l, sip_l, srs_l, sis_l = [], [], [], []
    cvp_l, cvs_l = [], []
    for i in range(DEPTH):
        shp1, scp1, gtp1, shp2, scp2, gtp2 = ada_mod(c_prompt, w_ada[i], b_ada[i])
        shs1, scs1, gts1, shs2, scs2, gts2 = ada_mod(c_sample, w_ada[i], b_ada[i])
        hp = xp * (1.0 + scp1) + shp1
        hs = xs * (1.0 + scs1) + shs1
        j = i // 2
        if i % 2 == 0:
            disc = ssm_discretise(ssm_lam_re[j], ssm_lam_im[j], ssm_log_step[j], ssm_b_re[j], ssm_b_im[j])
            zp = jnp.zeros((bp, N_GROUPS, STATE_P), jnp.float32)
            mp, srp, sip = ssm_mixer(hp, zp, zp, w_ssm_in[j], disc, ssm_c_re[j], ssm_c_im[j], ssm_d[j], w_glu[j])
            ms, srs, sis = ssm_mixer(hs, state_ssm_re[j], state_ssm_im[j], w_ssm_in[j], disc,
                                     ssm_c_re[j], ssm_c_im[j], ssm_d[j], w_glu[j])
            srp_l.append(srp); sip_l.append(sip); srs_l.append(srs); sis_l.append(sis)
        else:
            lam_init = 0.8 - 0.6 * math.exp(-0.3 * i)
            lam = diff_lambda(lam_q1[j], lam_k1[j], lam_q2[j], lam_k2[j], lam_init)
            mp, kp, vp = diff_attn_prompt(hp, w_qkv[j], lam, lam_init, subln_g[j], w_o[j])
            ms, kn, vn = diff_attn_sample(hs, cache_k[j], cache_v[j], w_qkv[j], lam, lam_init, subln_g[j], w_o[j])
            kp_l.append(kp); vp_l.append(vp); ks_l.append(kn); vs_l.append(vn)
        xp = layer_norm(ALPHA * xp + (1.0 + gtp1) * mp, ln_g[i, 0], ln_b[i, 0])
        xs = layer_norm(ALPHA * xs + (1.0 + gts1) * ms, ln_g[i, 0], ln_b[i, 0])
        hp = xp * (1.0 + scp2) + shp2
        hs = xs * (1.0 + scs2) + shs2
        fp, cvp = conv_ffn(hp, jnp.zeros((bp, CONV_W - 1, D_FF), xp.dtype), w_up[i], w_dconv[i], b_dconv[i], w_down[i])
        fs, cvs = conv_ffn(hs, state_conv[i], w_up[i], w_dconv[i], b_dconv[i], w_down[i])
        cvp_l.append(cvp); cvs_l.append(cvs)
        xp = layer_norm(ALPHA * xp + (1.0 + gtp2) * fp, ln_g[i, 1], ln_b[i, 1])
        xs = layer_norm(ALPHA * xs + (1.0 + gts2) * fs, ln_g[i, 1], ln_b[i, 1])
    new_k_prompt = jnp.stack(kp_l)
    new_v_prompt = jnp.stack(vp_l)
    new_ssm_re_prompt = jnp.stack(srp_l)
    new_ssm_im_prompt = jnp.stack(sip_l)
    new_conv_prompt = jnp.stack(cvp_l)
    new_k_sample = jnp.stack(ks_l)
    new_v_sample = jnp.stack(vs_l)
    new_ssm_re_sample = jnp.stack(srs_l)
    new_ssm_im_sample = jnp.stack(sis_l)
    new_conv_sample = jnp.stack(cvs_l)
    return (xp, xs, new_k_prompt, new_v_prompt, new_ssm_re_prompt, new_ssm_im_prompt, new_conv_prompt,
            new_k_sample, new_v_sample, new_ssm_re_sample, new_ssm_im_sample, new_conv_sample)
```

```python
import math
from contextlib import ExitStack
import numpy as np
import concourse.bass as bass
import concourse.mybir as mybir
from concourse.bass_utils import run_bass_kernel_spmd

F32 = mybir.dt.float32
BF16 = mybir.dt.bfloat16
I32 = mybir.dt.int32
AF = mybir.ActivationFunctionType
ALU = mybir.AluOpType
ROT = 12000

D = 2048
DC = 16
DFF = 5632
FC = 44
TT = 512
NH = 8
PAST = 2048
ALPHA = 4.0 ** 0.25
LN_EPS = 1e-5
EPS_LN = LN_EPS / (ALPHA * ALPHA)
LAM_INIT = 0.8 - 0.6 * math.exp(-0.3 * 1)


class _Stop(Exception):
    pass


class Buf:
    __slots__ = ("name", "lw", "rd", "sem", "cnt")

    def __init__(self, name):
        self.name = name
        self.lw = None
        self.rd = []
        self.sem = None
        self.cnt = 0


class FW:
    ENGS = ("pe", "act", "dve", "pool", "sp")

    def __init__(self, nc, stack):
        self.nc = nc
        self.stack = stack
        self.prog = {e: [] for e in self.ENGS}
        self.count = {e: 0 for e in self.ENGS}
        self.waited = {e: {} for e in self.ENGS}
        self.force = {e: [] for e in self.ENGS}
        self.sems = {}
        self.out_tokens = []
        self.pending = []
        self.fence = Buf("fence")
        self.dummy = None
        self.stopped = False
        self.strict = False
        self.strict_default = False

    def _sem(self, key):
        s = self.sems.get(key)
        if s is None:
            s = self.stack.enter_context(self.nc.semaphore("s_" + "_".join(str(k) for k in key)))
            self.sems[key] = s
        return s

    def _tok_eng(self, eng):
        n = self.count[eng]
        return (("e", eng, (n - 1) // ROT), (n - 1) % ROT + 1)

    def _need(self, eng, tok, waits):
        if tok is None:
            return
        key, val = tok
        if key[0] == "e" and key[1] == eng and not self.strict:
            return
        w = self.waited[eng]
        if w.get(key, 0) >= val:
            return
        w[key] = val
        waits.append((key, val))

    def _deps(self, eng, reads, writes):
        waits = []
        for t in self.force[eng]:
            self._need(eng, t, waits)
        self.force[eng] = []
        for b in reads:
            self._need(eng, b.lw, waits)
        for b in writes:
            self._need(eng, b.lw, waits)
            for t in b.rd:
                self._need(eng, t, waits)
        return waits

    def op(self, eng, fn, reads=(), writes=(), strict=None):
        if self.stopped:
            return None
        self.strict = self.strict_default if strict is None else strict
        waits = self._deps(eng, reads, writes)
        self.strict = False
        self.count[eng] += 1
        tok = self._tok_eng(eng)
        self._sem(tok[0])
        for b in writes:
            b.lw = tok
            b.rd = []
        for b in reads:
            b.rd = [t for t in b.rd if t[0] != tok[0]] + [tok]
        self.prog[eng].append((waits, fn, tok))
        return tok

    def dma(self, eng, fn, reads=(), writes=(), key=None, is_output=False, ring=False):
        if self.stopped:
            return None
        kb = key
        if kb.sem is None:
            kb.sem = ("d", kb.name)
            self._sem(kb.sem)
        saved = []
        for b in writes:
            if b.lw is not None and b.lw[0] == kb.sem and not b.rd:
                saved.append((b, b.lw))
                b.lw = None
        waits = self._deps(eng, reads, writes)
        for b, t in saved:
            b.lw = t
        kb.cnt += 16
        tok = (kb.sem, kb.cnt)
        for b in writes:
            b.lw = tok
            b.rd = []
        for b in reads:
            b.rd = b.rd + [tok]
        self.prog[eng].append((waits, fn, tok))
        if is_output:
            self.out_tokens.append(tok)
        if not ring:
            self.pending.append(tok)
        return tok

    def barrier(self, engines=("pe", "dve")):
        if self.stopped:
            return
        waits = self._deps("act", (), ())
        for eng in engines:
            if self.count[eng] > 0:
                self._need("act", self._tok_eng(eng), waits)
        for tok in self.pending:
            self._need("act", tok, waits)
        self.pending = []
        self.count["act"] += 1
        tok = self._tok_eng("act")
        self._sem(tok[0])
        d = self.dummy
        self.prog["act"].append((waits, lambda e: e.activation(out=d[:, 0:1], in_=d[:, 1:2], func=AF.Copy), tok))
        for eng in engines:
            self.force[eng].append(tok)
        self.fence.lw = tok
        self.fence.rd = []

    def finish(self, eng="sp"):
        waits = []
        for tok in self.out_tokens:
            self._need(eng, tok, waits)
        for tok in self.pending:
            self._need(eng, tok, waits)
        self.prog[eng].append((waits, None, None))

    def emit(self):
        nc = self.nc
        hmap = {"pe": "tensor", "act": "scalar", "dve": "vector", "pool": "gpsimd", "sp": "sync"}
        with nc.Block() as block:
            for eng in self.ENGS:
                prog = self.prog[eng]
                if not prog:
                    continue

                def body(e, prog=prog):
                    for waits, fn, tok in prog:
                        for key, val in waits:
                            e.wait_ge(self.sems[key], val)
                        if fn is None:
                            continue
                        inst = fn(e)
                        inst.then_inc(self.sems[tok[0]], 1 if tok[0][0] == "e" else 16)

                getattr(block, hmap[eng])(body)


def build(n_tiles=8, do_sample=True, stop_after=None, debug=False):
    nc = bass.Bass("TRN2", target_bir_lowering=False)
    SEQ = TT * n_tiles

    def din(name, shape, dtype=F32):
        return nc.dram_tensor(name, list(shape), dtype, kind="ExternalInput").ap()

    def dout(name, shape, dtype=F32):
        return nc.dram_tensor(name, list(shape), dtype, kind="ExternalOutput").ap()

    def dscr(name, shape, dtype):
        return nc.dram_tensor(name, list(shape), dtype, kind="Internal").ap()

    x_p = din("x_p", [SEQ, D]); x_s = din("x_s", [64, D]); c_all = din("c_all", [3, D])
    cache_k = din("cache_k", [2, PAST, D]); cache_v = din("cache_v", [2, PAST, D])
    st_re = din("st_re", [2, 64, 128]); st_im = din("st_im", [2, 64, 128])
    st_conv = din("st_conv", [2, 2, 2, DFF])
    w_ada = din("w_ada", [2, D, 6 * D]); b_ada = din("b_ada", [2, 96, 128])
    ln_g = din("ln_g", [64, 128]); ln_b = din("ln_b", [64, 128])
    w_up = din("w_up", [2, D, 2 * DFF]); w_dconv = din("w_dconv", [264, 128]); b_dconv = din("b_dconv", [88, 128])
    w_down = din("w_down", [2, DFF, D]); w_in = din("w_in", [D, D])
    lam_re = din("lam_re", [128, 64]); lam_im = din("lam_im", [128, 64]); log_step = din("log_step", [128])
    b_re = din("b_re", [128, 1024]); b_im = din("b_im", [128, 1024])
    c_re = din("c_re", [2048, 64]); c_im = din("c_im", [2048, 64]); ssm_d = din("ssm_d", [16, 128])
    w_glu = din("w_glu", [D, 2 * D]); w_qkv = din("w_qkv", [D, 3 * D]); lamv = din("lamv", [4, 128])
    subln = din("subln", [2, 128]); w_o = din("w_o", [D, D])

    y_p = dout("y_p", [SEQ, D]); y_s = dout("y_s", [64, D])
    k_p = dout("k_p", [SEQ, D]); v_p = dout("v_p", [SEQ, D])
    sre_p = dout("sre_p", [64, 128]); sim_p = dout("sim_p", [64, 128]); conv_p = dout("conv_p", [2, 2, DFF])
    k_s = dout("k_s", [64, D]); v_s = dout("v_s", [64, D])
    sre_s = dout("sre_s", [2, 64, 128]); sim_s = dout("sim_s", [2, 64, 128]); conv_s = dout("conv_s", [2, 2, 2, DFF])

    if debug:
        dbg_p = dout("dbg_p", [4, SEQ, D]); dbg_s = dout("dbg_s", [4, 64, D])
    kT_scr = dscr("kT_scr", [16, 128, SEQ], BF16)
    v_scr = dscr("v_scr", [SEQ, D], BF16)
    bb_scr = dscr("bb_scr", [2, 2048, 64], F32)
    bw_scr = dscr("bw_scr", [128, 2 * 64 * 128], BF16)
    cw_scr = dscr("cw_scr", [128, 2 * 64 * 64], BF16)
    wscr = dscr("wscr", [176, 128, 5632], BF16)

    st = ExitStack()
    with st:
        fw = FW(nc, st)

        def sb(name, shape, dtype):
            return st.enter_context(nc.sbuf_tensor(name, list(shape), dtype))

        def ck(name):
            if stop_after == name:
                fw.stopped = True

        xT = sb("xT", [128, DC, TT], F32)
        hB = sb("hB", [128, DC, TT], BF16)
        Rr = sb("Rr", [128, 8192], F32)
        BIG = sb("BIG", [128, 24576], BF16)
        TMP = sb("TMP", [128, 6144], F32)
        wr = [sb(f"wr{i}", [128, 5632], BF16) for i in range(3)]
        ident = sb("ident", [128, 128], F32)
        ones_b = sb("ones_b", [128, 128], BF16)
        Dg = sb("Dg", [128, 16, 128], BF16)
        dummy = sb("dummyt", [128, 2], F32)
        modt = sb("modt", [128, 2, 96, 3], F32)
        A1m = sb("A1m", [128, 16, 3], F32)
        G1 = sb("G1", [128, 2, 16, 3], F32)
        G2 = sb("G2", [128, 2, 16, 3], F32)
        HS2 = sb("HS2", [128, 2, 16, 3], F32); HB2 = sb("HB2", [128, 2, 16, 3], F32)
        HS1n = sb("HS1n", [128, 16, 3], F32); HB1n = sb("HB1n", [128, 16, 3], F32)
        lng = sb("lng", [128, 64], F32); lnb = sb("lnb", [128, 64], F32)
        bada = sb("bada", [128, 2, 96], F32)
        wdc = sb("wdc", [128, 264], F32); bdc = sb("bdc", [128, 88], F32)
        dsk = sb("dsk", [128, 16], F32)
        ccar = sb("ccar", [128, 2, 3, FC, 2], F32)
        A1s = sb("A1s", [128, 2, 64], F32); A2s = sb("A2s", [128, 2, 64], F32)
        Scar = sb("Scar", [128, 3, 2, 64], F32)
        lamt = sb("lamt", [128, 4], F32)
        sgs = sb("sgs", [128, 2], F32)
        cTb = sb("cTb", [128, 16, 3], BF16)
        ps = [st.enter_context(nc.psum_tensor(f"ps{i}", [128, 512], F32)) for i in range(8)]
        fw.dummy = dummy

        PB = [Buf(f"ps{i}") for i in range(8)]
        XT = [Buf(f"xT{c}") for c in range(DC)]
        HBb = [Buf(f"hB{c}") for c in range(DC)]
        WB = [Buf(f"wr{i}") for i in range(3)]
        bR = Buf("R"); bTMP = Buf("TMP"); bBIG = Buf("BIG")
        bConst = Buf("const")
        bKscr = Buf("kscr"); bVscr = Buf("vscr")
        bufs = {}

        def B(name):
            b = bufs.get(name)
            if b is None:
                b = Buf(name)
                bufs[name] = b
            return b

        def bigv(off_bf16, n):
            return BIG[:, off_bf16:off_bf16 + n]

        actT = BIG[:, 0:FC * TT].rearrange("p (c t) -> p c t", t=TT)
        uB = BIG[:, 0:8192].rearrange("p (c t) -> p c t", t=TT)
        HbT = Rr[:, 4096:6144].bitcast(BF16).rearrange("p (r j t) -> p r j t", r=2, j=64)
        CwT = TMP[:, 0:4096].bitcast(BF16).rearrange("p (r j k) -> p r j k", r=2, j=64)
        CwI = BIG[:, 0:8192].rearrange("p (r j k) -> p r j k", r=2, j=64)
        KTt = BIG[:, 0:8192].rearrange("p (s t) -> p s t", s=2)
        Vht = BIG[:, 8192:16384].rearrange("p (k e) -> p k e", e=256)
        qT = BIG[:, 16384:24576].rearrange("p (c t) -> p c t", t=TT)
        Bt = Rr[:, 0:4096].rearrange("p (r j t) -> p r j t", r=2, j=64)
        kTf = Rr[:, 0:8192].rearrange("p (c t) -> p c t", t=TT)
        BwT = BIG[:, 8192:24576].rearrange("p (r j k) -> p r j k", r=2, j=64)
        stg = [TMP[:, 0:2048], TMP[:, 2048:4096]]
        bStg = [B("stg0"), B("stg1")]

        def mm(out, lhsT, rhs, start, stop, reads, writes):
            fw.op("pe", lambda e: e.matmul(out, lhsT=lhsT, rhs=rhs, start=start, stop=stop), reads=reads, writes=writes)

        def tp(out, in_, rows, reads, writes):
            fw.op("pe", lambda e: e.transpose(out=out, in_=in_, identity=ident[0:rows, 0:rows]),
                  reads=list(reads) + [bConst], writes=writes)

        def act(out, in_, func, reads, writes, bias=None, scale=None, strict=None):
            kw = {}
            if bias is not None:
                kw["bias"] = bias
            if scale is not None:
                kw["scale"] = scale
            fw.op("act", lambda e: e.activation(out=out, in_=in_, func=func, **kw), reads=reads, writes=writes,
                  strict=strict)

        def dve(fn, reads, writes, strict=None):
            fw.op("dve", fn, reads=reads, writes=writes, strict=strict)

        def tt(out, in0, in1, op, reads, writes, strict=None):
            dve(lambda e: e.tensor_tensor(out=out, in0=in0, in1=in1, op=op), reads, writes, strict)

        def ts(out, in0, s1, s2, op0, op1, reads, writes, strict=None):
            dve(lambda e: e.tensor_scalar(out=out, in0=in0, scalar1=s1, scalar2=s2, op0=op0, op1=op1), reads, writes,
                strict)

        def stt(out, in0, scalar, in1, op0, op1, reads, writes, strict=None):
            dve(lambda e: e.scalar_tensor_tensor(out=out, in0=in0, scalar=scalar, in1=in1, op0=op0, op1=op1),
                reads, writes, strict)

        def sp_dma(out, in_, reads, writes, key, is_output=False, nc_ok=False):
            fw.dma("sp", lambda e: e.dma_start(out=out, in_=in_, allow_slow_non_contiguous=nc_ok),
                   reads=list(reads) + [fw.fence], writes=writes, key=key, is_output=is_output)

        pst = [0]

        def tps():
            pst[0] ^= 1
            return 6 + pst[0]

        def load_T(dst, src, rows):
            k = tps()
            sp_dma(stg[k - 6][0:rows, 0:128], src, [], [bStg[k - 6]], key=bStg[k - 6])
            tp(ps[k][:, 0:rows], stg[k - 6][0:rows, 0:128], rows, [bStg[k - 6]], [PB[k]])
            act(dst, ps[k][:, 0:rows], AF.Copy, [PB[k]], [bConst])

        def store_T(dst, src, rows, rbufs, is_output=True):
            k = tps()
            tp(ps[k][0:rows, 0:128], src, 128, rbufs, [PB[k]])
            act(stg[k - 6][0:rows, 0:128], ps[k][0:rows, 0:128], AF.Copy, [PB[k]], [bStg[k - 6]])
            sp_dma(dst, stg[k - 6][0:rows, 0:128], [bStg[k - 6]], [], key=bStg[k - 6], is_output=is_output)

        wseq = []
        wpos = [0, 0]

        def wsrc(kind, a):
            if kind == "ada":
                l, b = a
                return w_ada[l].rearrange("(kc p) n -> p kc n", p=128)[:, :, b * 256:(b + 1) * 256], (16, 256)
            if kind == "in":
                return w_in.rearrange("(kc p) n -> p kc n", p=128)[:, :, a * 256:(a + 1) * 256], (16, 256)
            if kind == "glu":
                return (w_glu.rearrange("(kc p) (two n) -> p kc two n", p=128, two=2)[:, :, :, a * 128:(a + 1) * 128],
                        (16, 2, 128))
            if kind == "up":
                l, f = a
                return (w_up[l].rearrange("(kc p) (two n) -> p kc two n", p=128, two=2)[:, :, :, f * 128:(f + 1) * 128],
                        (16, 2, 128))
            if kind == "down":
                l, m = a
                return w_down[l].rearrange("(kc p) n -> p kc n", p=128)[:, :, m * 128:(m + 1) * 128], (44, 128)
            if kind == "qkv":
                return w_qkv.rearrange("(kc p) n -> p kc n", p=128)[:, :, a * 256:(a + 1) * 256], (16, 256)
            if kind == "o":
                return w_o.rearrange("(kc p) n -> p kc n", p=128)[:, :, a * 256:(a + 1) * 256], (16, 256)
            raise ValueError(kind)

        def wview(slot, shp):
            n = int(np.prod(shp))
            v = wr[slot][:, 0:n]
            if len(shp) == 2:
                return v.rearrange("p (k n) -> p k n", n=shp[1])
            return v.rearrange("p (k two n) -> p k two n", two=shp[1], n=shp[2])

        WS = [Buf(f"ws{i}") for i in range(3)]
        Wscr = [Buf(f"wscr{i}") for i in range(176)]

        def w_issue():
            i = wpos[1]
            if i >= len(wseq):
                return
            kind, a = wseq[i]
            src, shp = wsrc(kind, a)
            slot = i % 3
            dstv = wview(slot, shp)
            n = int(np.prod(shp))
            tn, bi = ((i - 96) // 176, (i - 96) % 176) if kind != "ada" else (0, -1)
            if kind != "ada" and tn >= 1:
                fw.dma("pool", lambda e: e.dma_start(out=wr[slot][:, 0:n], in_=wscr[bi][:, 0:n]),
                       reads=[Wscr[bi]], writes=[WB[slot]], key=WB[slot], ring=True)
            else:
                if len(shp) == 2:
                    fw.dma("pool", lambda e: e.dma_start(out=dstv, in_=src), reads=[], writes=[WB[slot]], key=WB[slot],
                           ring=True)
                else:
                    for two in range(2):
                        fw.dma("pool", lambda e, two=two: e.dma_start(out=dstv[:, :, two, :], in_=src[:, :, two, :]),
                               reads=[], writes=[WB[slot]], key=WB[slot], ring=True)
                if kind != "ada" and len(tiles) > 1:
                    fw.dma("sp", lambda e: e.dma_start(out=wscr[bi][:, 0:n], in_=wr[slot][:, 0:n]),
                           reads=[WB[slot]], writes=[Wscr[bi]], key=WS[slot])
            wpos[1] += 1

        def wget(kind, a):
            i = wpos[0]
            assert wseq[i] == (kind, a), (wseq[i], kind, a)
            while wpos[1] < min(i + 3, len(wseq)):
                w_issue()
            wpos[0] += 1
            slot = i % 3
            _, shp = wsrc(kind, a)
            return wview(slot, shp), WB[slot]

        tiles = [("p", i) for i in range(n_tiles)] + ([("s", 0)] if do_sample else [])
        for l in range(2):
            for b in range(48):
                wseq.append(("ada", (l, b)))
        for _ in tiles:
            for b in range(8):
                wseq.append(("in", b))
            for m in range(16):
                wseq.append(("glu", m))
            for f in range(FC):
                wseq.append(("up", (0, f)))
            for m in range(16):
                wseq.append(("down", (0, m)))
            for b in range(24):
                wseq.append(("qkv", b))
            for b in range(8):
                wseq.append(("o", b))
            for f in range(FC):
                wseq.append(("up", (1, f)))
            for m in range(16):
                wseq.append(("down", (1, m)))

        fw.strict_default = True
        fw.op("pool", lambda e: e.memset(ident[:], 0.0), writes=[bConst])
        fw.op("pool", lambda e: e.affine_select(out=ident[:], in_=ident[:], pattern=[[-1, 128]],
                                                compare_op=ALU.not_equal, fill=1.0, base=0, channel_multiplier=1),
              reads=[bConst], writes=[bConst])
        fw.op("pool", lambda e: e.memset(ones_b[:], 1.0), writes=[bConst])
        fw.op("pool", lambda e: e.memset(dummy[:], 0.0), writes=[bConst])
        fw.op("pool", lambda e: e.memset(ccar[:], 0.0), writes=[bConst])
        fw.op("pool", lambda e: e.memset(Scar[:], 0.0), writes=[bConst])
        fw.barrier(engines=("pe", "dve", "pool"))

        load_T(lng[:, 0:64], ln_g, 64)
        load_T(lnb[:, 0:64], ln_b, 64)
        for l in range(2):
            load_T(bada[:, l, :], b_ada[l], 96)
        for i in range(3):
            load_T(wdc[:, i * 88:(i + 1) * 88], w_dconv[i * 88:(i + 1) * 88, :], 88)
        load_T(bdc[:, 0:88], b_dconv, 88)
        load_T(dsk[:, 0:16], ssm_d, 16)
        load_T(sgs[:, 0:2], subln, 2)
        ts(sgs[:], sgs[:], 1.0 - LAM_INIT, None, ALU.mult, ALU.bypass, [bConst], [bConst])
        for i in range(16):
            act(Dg[:, i, :], ident[:], AF.Copy, [bConst], [bConst], scale=dsk[:, i:i + 1])
        if do_sample:
            for q in range(2):
                load_T(Scar[:, 1 + q, 0, :], st_re[q], 64)
                load_T(Scar[:, 1 + q, 1, :], st_im[q], 64)
                for l in range(2):
                    for t2 in range(2):
                        load_T(ccar[:, l, 1 + q, :, t2], st_conv[l, q, t2].rearrange("(k f) -> k f", f=128), FC)

        lq = TMP[:, 4096:4096 + 512].rearrange("p (a b) -> p a b", a=4)
        sp_dma(lq, lamv.rearrange("a b -> (a b)").partition_broadcast(128).rearrange("p (a b) -> p a b", a=4),
               [], [bTMP], key=bTMP, nc_ok=True)
        tt(lq[:, 0, :], lq[:, 0, :], lq[:, 1, :], ALU.mult, [bTMP], [bTMP])
        tt(lq[:, 2, :], lq[:, 2, :], lq[:, 3, :], ALU.mult, [bTMP], [bTMP])
        dve(lambda e: e.reduce_sum(out=lamt[:, 0:1], in_=lq[:, 0, :], axis=mybir.AxisListType.X), [bTMP], [bConst])
        dve(lambda e: e.reduce_sum(out=lamt[:, 1:2], in_=lq[:, 2, :], axis=mybir.AxisListType.X), [bTMP], [bConst])
        act(lamt[:, 0:2], lamt[:, 0:2], AF.Exp, [bConst], [bConst])
        tt(lamt[:, 2:3], lamt[:, 1:2], lamt[:, 0:1], ALU.subtract, [bConst], [bConst])
        ts(lamt[:, 2:3], lamt[:, 2:3], -LAM_INIT, None, ALU.add, ALU.bypass, [bConst], [bConst])

        cs = Rr[0:3, 0:2048]
        sp_dma(cs, c_all, [], [bR], key=bR)
        act(cs, cs, AF.Silu, [bR], [bR])
        for kc in range(16):
            tp(ps[0][:, 3 * kc:3 * kc + 3], cs[0:3, kc * 128:(kc + 1) * 128], 3, [bR], [PB[0]])
        act(cTb[:].rearrange("p a b -> p (a b)"), ps[0][:, 0:48], AF.Copy, [PB[0]], [bConst])
        for l in range(2):
            for b in range(48):
                wv, wb = wget("ada", (l, b))
                for oc in range(2):
                    ch = 2 * b + oc
                    for kc in range(16):
                        mm(ps[1 + l][:, 3 * ch:3 * ch + 3], wv[:, kc, oc * 128:(oc + 1) * 128], cTb[:, kc, :],
                           kc == 0, kc == 15, [wb, bConst], [PB[1 + l]])
            tt(modt[:, l, :, :], ps[1 + l][:, 0:288].rearrange("p (c s) -> p c s", s=3),
               bada[:, l, :].unsqueeze(2).broadcast_to([128, 96, 3]), ALU.add, [PB[1 + l], bConst], [bConst])
        sh1 = lambda l: modt[:, l, 0:16, :]
        sc1 = lambda l: modt[:, l, 16:32, :]
        gt1 = lambda l: modt[:, l, 32:48, :]
        sh2 = lambda l: modt[:, l, 48:64, :]
        sc2 = lambda l: modt[:, l, 64:80, :]
        gt2 = lambda l: modt[:, l, 80:96, :]
        cc = [bConst]
        ts(A1m[:], sc1(0), 1.0, None, ALU.add, ALU.bypass, cc, cc)
        tmpm = TMP[:, 5120:5120 + 48].rearrange("p (c s) -> p c s", s=3)
        for l in range(2):
            ts(G1[:, l], gt1(l), 1.0, 1.0 / ALPHA, ALU.add, ALU.mult, cc, cc)
            ts(G2[:, l], gt2(l), 1.0, 1.0 / ALPHA, ALU.add, ALU.mult, cc, cc)
            g1 = lng[:, (l * 2 + 0) * 16:(l * 2 + 0) * 16 + 16].unsqueeze(2).broadcast_to([128, 16, 3])
            b1 = lnb[:, (l * 2 + 0) * 16:(l * 2 + 0) * 16 + 16].unsqueeze(2).broadcast_to([128, 16, 3])
            ts(tmpm, sc2(l), 1.0, None, ALU.add, ALU.bypass, cc, [bTMP])
            tt(HS2[:, l], tmpm, g1, ALU.mult, [bTMP] + cc, cc)
            tt(HB2[:, l], tmpm, b1, ALU.mult, [bTMP] + cc, cc)
            tt(HB2[:, l], HB2[:, l], sh2(l), ALU.add, cc, cc)
        g2 = lng[:, 16:32].unsqueeze(2).broadcast_to([128, 16, 3])
        b2 = lnb[:, 16:32].unsqueeze(2).broadcast_to([128, 16, 3])
        ts(tmpm, sc1(1), 1.0, None, ALU.add, ALU.bypass, cc, [bTMP])
        tt(HS1n[:], tmpm, g2, ALU.mult, [bTMP] + cc, cc)
        tt(HB1n[:], tmpm, b2, ALU.mult, [bTMP] + cc, cc)
        tt(HB1n[:], HB1n[:], sh1(1), ALU.add, cc, cc)
        fw.barrier()
        ck("s_params")

        def tmpv(i):
            return TMP[:, 4096 + 64 * i:4096 + 64 * (i + 1)]

        def sincos(dst, ang, shift, tA, tB, tI):
            ts(tA, ang, 1.0 / (2 * math.pi), 64.5 + shift, ALU.mult, ALU.add, [bTMP], [bTMP])
            dve(lambda e: e.tensor_copy(out=tI, in_=tA), [bTMP], [bTMP])
            dve(lambda e: e.tensor_copy(out=tB, in_=tI), [bTMP], [bTMP])
            tt(tA, tA, tB, ALU.subtract, [bTMP], [bTMP])
            dve(lambda e: e.tensor_single_scalar(out=tB, in_=tA, scalar=0.0, op=ALU.is_lt), [bTMP], [bTMP])
            tt(tA, tA, tB, ALU.add, [bTMP], [bTMP])
            ts(tA, tA, 2 * math.pi, -math.pi, ALU.mult, ALU.add, [bTMP], [bTMP])
            ts(tA, tA, 3.14159, -3.14159, ALU.min, ALU.max, [bTMP], [bTMP])
            act(dst, tA, AF.Sin, [bTMP], [bTMP])

        def disc(lr_in, li, dtt, o_are, o_aim, o_kr, o_ki, base):
            lr, prod, mag, cs_, sn_, tA, tB = [tmpv(base + i) for i in range(7)]
            tI = tmpv(base + 7).bitcast(I32)
            nr = tmpv(base + 8)
            ts(lr, lr_in, -1e-4, None, ALU.min, ALU.bypass, [bTMP], [bTMP])
            tt(prod, lr, dtt, ALU.mult, [bTMP], [bTMP])
            act(mag, prod, AF.Exp, [bTMP], [bTMP])
            tt(prod, li, dtt, ALU.mult, [bTMP], [bTMP])
            sincos(sn_, prod, 0.0, tA, tB, tI)
            sincos(cs_, prod, 0.25, tA, tB, tI)
            tt(o_are, mag, cs_, ALU.mult, [bTMP], [bTMP, bConst])
            tt(o_aim, mag, sn_, ALU.mult, [bTMP], [bTMP, bConst])
            if o_kr is None:
                return
            ts(nr, o_are, -1.0, None, ALU.add, ALU.bypass, [bTMP], [bTMP])
            tt(tA, lr, lr, ALU.mult, [bTMP], [bTMP])
            tt(tB, li, li, ALU.mult, [bTMP], [bTMP])
            tt(tA, tA, tB, ALU.add, [bTMP], [bTMP])
            dve(lambda e: e.reciprocal(out=tA, in_=tA), [bTMP], [bTMP])
            tt(o_kr, nr, lr, ALU.mult, [bTMP], [bTMP])
            tt(tB, o_aim, li, ALU.mult, [bTMP], [bTMP])
            tt(o_kr, o_kr, tB, ALU.add, [bTMP], [bTMP])
            tt(o_kr, o_kr, tA, ALU.mult, [bTMP], [bTMP])
            tt(o_ki, o_aim, lr, ALU.mult, [bTMP], [bTMP])
            tt(tB, nr, li, ALU.mult, [bTMP], [bTMP])
            tt(o_ki, o_ki, tB, ALU.subtract, [bTMP], [bTMP])
            tt(o_ki, o_ki, tA, ALU.mult, [bTMP], [bTMP])

        lrS, liS, dtS = tmpv(20), tmpv(21), tmpv(22)
        load_T(lrS, lam_re.rearrange("(j gg) p -> j (gg p)", gg=2), 64)
        load_T(liS, lam_im.rearrange("(j gg) p -> j (gg p)", gg=2), 64)
        k = tps()
        sp_dma(stg[k - 6][0:64, 256:258], log_step.rearrange("(j gg) -> j gg", gg=2), [], [bStg[k - 6]], key=bStg[k - 6])
        dve(lambda e, k=k: e.tensor_copy(out=stg[k - 6][0:64, 0:128].rearrange("j (gg p) -> j gg p", gg=2),
                                         in_=stg[k - 6][0:64, 256:258].unsqueeze(2).broadcast_to([64, 2, 64])),
            [bStg[k - 6]], [bStg[k - 6]])
        tp(ps[k][:, 0:64], stg[k - 6][0:64, 0:128], 64, [bStg[k - 6]], [PB[k]])
        act(dtS, ps[k][:, 0:64], AF.Exp, [PB[k]], [bTMP])
        fw.barrier()
        ck("s_ada")
        disc(lrS, liS, dtS, A1s[:, 0, :], A2s[:, 0, :], None, None, 0)
        dve(lambda e: e.tensor_copy(out=A1s[:, 1, :], in_=A1s[:, 0, :]), [bConst], [bConst])
        dve(lambda e: e.tensor_copy(out=A2s[:, 1, :], in_=A2s[:, 0, :]), [bConst], [bConst])
        lrA, liA, dtA, krA, kiA, areA, aimA = [tmpv(23 + i) for i in range(7)]
        sp_dma(lrA, lam_re, [], [bTMP], key=bTMP)
        sp_dma(liA, lam_im, [], [bTMP], key=bTMP)
        sp_dma(dtA[:, 0:1], log_step.unsqueeze(1), [], [bTMP], key=bTMP)
        act(dtA[:, 1:2], dtA[:, 0:1], AF.Exp, [bTMP], [bTMP])
        dve(lambda e: e.tensor_copy(out=dtA, in_=dtA[:, 1:2].broadcast_to([128, 64])), [bTMP], [bTMP])
        disc(lrA, liA, dtA, areA, aimA, krA, kiA, 0)
        brT = Rr[:, 0:1024].rearrange("g (p c) -> g p c", c=16)
        biT = Rr[:, 1024:2048].rearrange("g (p c) -> g p c", c=16)
        oR = Rr[:, 2048:3072].rearrange("g (c p) -> g p c", p=64)
        oI = Rr[:, 3072:4096].rearrange("g (c p) -> g p c", p=64)
        t1 = Rr[:, 4096:5120].rearrange("g (p c) -> g p c", c=16)
        sp_dma(Rr[:, 0:1024], b_re, [], [bR], key=bR)
        sp_dma(Rr[:, 1024:2048], b_im, [], [bR], key=bR)
        krB = krA.unsqueeze(2).broadcast_to([128, 64, 16])
        kiB = kiA.unsqueeze(2).broadcast_to([128, 64, 16])
        tt(oR, brT, krB, ALU.mult, [bR, bTMP], [bR])
        tt(t1, biT, kiB, ALU.mult, [bR, bTMP], [bR])
        tt(oR, oR, t1, ALU.subtract, [bR], [bR])
        tt(oI, biT, krB, ALU.mult, [bR, bTMP], [bR])
        tt(t1, brT, kiB, ALU.mult, [bR, bTMP], [bR])
        tt(oI, oI, t1, ALU.add, [bR], [bR])
        bBB = B("bb_scr")
        sp_dma(bb_scr[0].rearrange("(g c) p -> g (c p)", c=16), Rr[:, 2048:3072], [bR], [bBB], key=bR)
        sp_dma(bb_scr[1].rearrange("(g c) p -> g (c p)", c=16), Rr[:, 3072:4096], [bR], [bBB], key=bR)
        fw.barrier()
        ck("s_disc")
        dve(lambda e: e.memset(BIG[:, 8192:24576], 0.0), [bBIG], [bBIG])
        for r in range(2):
            srcv = bb_scr[r].rearrange("(i P) p -> P i p", P=128)
            for q4 in range(4):
                for gg in range(2):
                    p0 = 16 * (2 * q4 + gg)
                    dstv = BwT[p0:p0 + 16, r, :, :].rearrange("p (i q) k -> p i q k", q=4)[:, :, q4, gg * 64:(gg + 1) * 64]
                    fw.dma("pool", lambda e, dstv=dstv, s_=srcv[p0:p0 + 16]: e.dma_start(out=dstv, in_=s_),
                           reads=[bBB, bBIG, fw.fence], writes=[bBIG], key=bBIG)
        bBW = B("bw_scr")
        sp_dma(bw_scr, BIG[:, 8192:24576], [bBIG], [bBW], key=bBIG)
        dve(lambda e: e.memset(BIG[:, 0:8192], 0.0), [bBIG], [bBIG])
        CTs = Rr[0:64, 4096:8192].rearrange("p (r i k) -> p r i k", r=2, i=16)
        for r, csrc in enumerate((c_re, c_im)):
            for i in range(16):
                k = tps()
                sp_dma(stg[k - 6][:, 0:64], csrc[i * 128:(i + 1) * 128, :], [], [bStg[k - 6]], key=bStg[k - 6])
                tp(ps[k][0:64, 0:128], stg[k - 6][:, 0:64], 128, [bStg[k - 6]], [PB[k]])
                act(CTs[:, r, i, :], ps[k][0:64, 0:128], AF.Copy, [PB[k]], [bR], scale=(1.0 if r == 0 else -1.0))
        for r in range(2):
            for gg in range(2):
                for qq in range(2):
                    cb = (2 * qq + gg) * 16
                    dstv = CwI[64 * gg:64 * gg + 64, r].rearrange("p (i hf q) k -> p i hf q k", hf=2, q=2)[:, :, :, qq, cb:cb + 16]
                    srcv = CTs[:, r].rearrange("p i (hf q g c) -> p i hf q g c", hf=2, q=2, g=2)[:, :, :, qq, gg, :]
                    fw.dma("pool", lambda e, dstv=dstv, srcv=srcv: e.dma_start(out=dstv, in_=srcv),
                           reads=[bR, bBIG, fw.fence], writes=[bBIG], key=bBIG)
        bCW = B("cw_scr")
        sp_dma(cw_scr, BIG[:, 0:8192], [bBIG], [bCW], key=bBIG)
        fw.barrier()
        ck("s_ssmw")
        fw.strict_default = False

        def ln_block(l, k, T, segs, producer, gate, last_layer):
            S1, S2 = 4, 5
            stat = TMP[:, 4096:6144].rearrange("p (a t) -> p a t", a=4)
            for m in range(DC):
                src, sbufs = producer(m)
                for (c0, c1, s) in segs:
                    stt(xT[:, m, c0:c1], src[:, c0:c1], gate[:, m, s:s + 1], xT[:, m, c0:c1], ALU.mult, ALU.add,
                        sbufs + [XT[m], bConst], [XT[m]])
                rb = TMP[:, 2048 + (m % 2) * 512:2048 + (m % 2) * 512 + 256].bitcast(BF16)
                rq = TMP[:, 2048 + (m % 2) * 512 + 256:2048 + (m % 2) * 512 + 512].bitcast(BF16)
                brb = B(f"rb{m % 2}")
                act(rb[:, 0:T], xT[:, m, 0:T], AF.Copy, [XT[m]], [brb])
                mm(ps[S1][:, 0:T], ones_b[:], rb[:, 0:T], m == 0, m == DC - 1, [brb, bConst], [PB[S1]])
                brq = B(f"rq{m % 2}")
                act(rq[:, 0:T], xT[:, m, 0:T], AF.Square, [XT[m]], [brq])
                mm(ps[S2][:, 0:T], ones_b[:], rq[:, 0:T], m == 0, m == DC - 1, [brq, bConst], [PB[S2]])
            bst = B("lnstat")
            mean, var, nmr = stat[:, 0, 0:T], stat[:, 1, 0:T], stat[:, 2, 0:T]
            ts(mean, ps[S1][:, 0:T], 1.0 / D, None, ALU.mult, ALU.bypass, [PB[S1]], [bst], True)
            tt(nmr, mean, mean, ALU.mult, [bst], [bst], True)
            stt(var, ps[S2][:, 0:T], 1.0 / D, nmr, ALU.mult, ALU.subtract, [PB[S2], bst], [bst], True)
            ts(var, var, EPS_LN, None, ALU.add, ALU.bypass, [bst], [bst], True)
            act(var, var, AF.Sqrt, [bst], [bst], strict=True)
            dve(lambda e: e.reciprocal(out=var, in_=var), [bst], [bst], True)
            rstd_p, nmr_p = ps[S2][:, 0:T], ps[S1][:, 0:T]
            dve(lambda e: e.tensor_copy(out=rstd_p, in_=var), [bst], [PB[S2]], True)
            stt(nmr_p, mean, -1.0, var, ALU.mult, ALU.mult, [bst], [PB[S1]], True)
            gi = (l * 2 + k) * 16
            for m in range(DC):
                tt(xT[:, m, 0:T], xT[:, m, 0:T], rstd_p, ALU.mult, [XT[m], PB[S2]], [XT[m]], m == 0)
                tt(xT[:, m, 0:T], xT[:, m, 0:T], nmr_p, ALU.add, [XT[m], PB[S1]], [XT[m]])
                if not (last_layer and k == 1):
                    for (c0, c1, s) in segs:
                        if k == 0:
                            hs, hb = HS2[:, l, m, s:s + 1], HB2[:, l, m, s:s + 1]
                        else:
                            hs, hb = HS1n[:, m, s:s + 1], HB1n[:, m, s:s + 1]
                        act(hB[:, m, c0:c1], xT[:, m, c0:c1], AF.Identity, [XT[m], bConst], [HBb[m]], bias=hb, scale=hs)
                act(xT[:, m, 0:T], xT[:, m, 0:T], AF.Identity, [XT[m], bConst], [XT[m]],
                    bias=lnb[:, gi + m:gi + m + 1], scale=lng[:, gi + m:gi + m + 1])

        def ffn(l, T, segs, last_layer):
            gb = [TMP[:, 0:520], TMP[:, 520:1040]]
            cb = [TMP[:, 1040:1552], TMP[:, 1552:2064]]
            geb = [TMP[:, 2064:2576], TMP[:, 2576:3088]]
            ACTb = [B(f"act{f}") for f in range(FC)]
            for f in range(FC):
                wv, wb = wget("up", (l, f))
                pg, pv = (f % 2) * 2, (f % 2) * 2 + 1
                for kc in range(16):
                    mm(ps[pg][:, 0:T], wv[:, kc, 0, :], hB[:, kc, 0:T], kc == 0, kc == 15, [wb, HBb[kc]], [PB[pg]])
                for kc in range(16):
                    mm(ps[pv][:, 0:T], wv[:, kc, 1, :], hB[:, kc, 0:T], kc == 0, kc == 15, [wb, HBb[kc]], [PB[pv]])
                g_, c_, ge_ = gb[f % 2], cb[f % 2], geb[f % 2]
                bg, bc, bge = B(f"gb{f % 2}"), B(f"cb{f % 2}"), B(f"geb{f % 2}")
                for si, (c0, c1, s) in enumerate(segs):
                    n = c1 - c0
                    o = c0 + 2 * si
                    act(g_[:, o + 2:o + 2 + n], ps[pg][:, c0:c1], AF.Copy, [PB[pg]], [bg])
                    dve(lambda e, o=o, s=s, f=f, g_=g_: e.tensor_copy(out=g_[:, o:o + 2], in_=ccar[:, l, s, f, :]),
                        [B("ccar")], [bg], True)
                    dve(lambda e, o=o, n=n, s=s, f=f, g_=g_: e.tensor_copy(out=ccar[:, l, s, f, :], in_=g_[:, o + n:o + n + 2]),
                        [bg], [B("ccar")], True)
                    wi = l * 132 + f
                    ts(c_[:, c0:c1], g_[:, o + 2:o + 2 + n], wdc[:, wi + 88:wi + 89], bdc[:, l * FC + f:l * FC + f + 1],
                       ALU.mult, ALU.add, [bg, bConst], [bc])
                    stt(c_[:, c0:c1], g_[:, o + 1:o + 1 + n], wdc[:, wi + 44:wi + 45], c_[:, c0:c1], ALU.mult, ALU.add,
                        [bg, bConst, bc], [bc], True)
                    stt(c_[:, c0:c1], g_[:, o:o + n], wdc[:, wi:wi + 1], c_[:, c0:c1], ALU.mult, ALU.add,
                        [bg, bConst, bc], [bc])
                act(ge_[:, 0:T], c_[:, 0:T], AF.Gelu, [bc], [bge])
                tt(actT[:, f, 0:T], ge_[:, 0:T], ps[pv][:, 0:T], ALU.mult, [bge, PB[pv]], [ACTb[f]])
            fw.barrier()
            ck("t_ffnup")

            def prod(m):
                wv, wb = wget("down", (l, m))
                pb = m % 4
                for kc in range(FC):
                    mm(ps[pb][:, 0:T], wv[:, kc, :], actT[:, kc, 0:T], kc == 0, kc == FC - 1, [wb, ACTb[kc]], [PB[pb]])
                return ps[pb], [PB[pb]]

            ln_block(l, 1, T, segs, prod, G2[:, l], last_layer)
            fw.barrier()
            ck("t_ffn")

        def load_x(src_rows, T):
            nchunk = max(1, T // 128)
            rows = min(T, 128)
            for tc in range(nchunk):
                k = tc % 2
                sp_dma(stg[k][0:rows, :], src_rows[tc * 128:tc * 128 + rows, :], [], [bStg[k]], key=bStg[k])
                for g4 in range(4):
                    pb = tps()
                    for q in range(4):
                        c = 4 * g4 + q
                        tp(ps[pb][:, q * 128:q * 128 + rows], stg[k][0:rows, c * 128:(c + 1) * 128], rows,
                           [bStg[k]], [PB[pb]])
                    fw.op("act", lambda e, pb=pb, g4=g4, tc=tc: e.activation(
                        out=xT[:, 4 * g4:4 * g4 + 4, tc * 128:tc * 128 + rows],
                        in_=ps[pb][:, :].rearrange("p (q t) -> p q t", t=128)[:, :, 0:rows], func=AF.Copy),
                        reads=[PB[pb]], writes=[XT[4 * g4 + q] for q in range(4)])

        def store_rows(dst_rows, srcT, src_bufs, T, is_output=True, extra=None):
            nchunk = max(1, T // 128)
            rows = min(T, 128)
            for tc in range(nchunk):
                k = tc % 2
                for g4 in range(4):
                    pb = tps()
                    for q in range(4):
                        c = 4 * g4 + q
                        tp(ps[pb][0:rows, q * 128:(q + 1) * 128], srcT[:, c, tc * 128:tc * 128 + rows], 128,
                           [src_bufs[c]], [PB[pb]])
                    act(stg[k][0:rows, g4 * 512:(g4 + 1) * 512], ps[pb][0:rows, :], AF.Copy, [PB[pb]], [bStg[k]])
                sp_dma(dst_rows[tc * 128:tc * 128 + rows, :], stg[k][0:rows, :], [bStg[k]], [], key=bStg[k],
                       is_output=is_output)

        def ssm_layer(T, segs, blocks):
            UB = [B(f"u{c}") for c in range(DC)]
            for b in range(8):
                wv, wb = wget("in", b)
                for oc in range(2):
                    m = 2 * b + oc
                    pb = m % 4
                    for kc in range(16):
                        mm(ps[pb][:, 0:T], wv[:, kc, oc * 128:(oc + 1) * 128], hB[:, kc, 0:T], kc == 0, kc == 15,
                           [wb, HBb[kc]], [PB[pb]])
                    act(uB[:, m, 0:T], ps[pb][:, 0:T], AF.Copy, [PB[pb]], [UB[m]])
            fw.barrier(engines=("pe", "dve", "pool"))
            ck("t_u")
            sp_dma(BIG[:, 8192:24576], bw_scr, [bBW], [bBIG], key=bBIG)
            sp_dma(TMP[:, 0:4096].bitcast(BF16), cw_scr, [bCW], [bTMP], key=bTMP)
            JD = 48
            T1 = TMP[:, 4096:4224].rearrange("p (r j) -> p r j", r=2)
            U_ = TMP[:, 4224:4352].rearrange("p (r j) -> p r j", r=2)
            bHb = B("Hb")
            halves = [("dve", 0, JD, B("T1d"), B("Ud"), B("Btd"), B("Scd")),
                      ("pool", JD, 64, B("T1p"), B("Up"), B("Btp"), B("Scp"))]
            bBt2 = [halves[0][5], halves[1][5]]
            def Bproj(bi):
                t0, n, sq = blocks[bi]
                for r in range(2):
                    for jg in range(4):
                        bank = (2 * r + jg) % 4
                        for jj in range(16):
                            j = jg * 16 + jj
                            mm(ps[bank][:, jj * 32:jj * 32 + n], BwT[:, r, j, :], uB[:, j // 4, t0:t0 + n], True, True,
                               [bBIG, UB[j // 4]], [PB[bank]])
                        fw.op("act", lambda e, bank=bank, r=r, j0=jg * 16, n=n: e.activation(
                            out=Bt[:, r, j0:j0 + 16, 0:n],
                            in_=ps[bank][:, :].rearrange("p (j t) -> p j t", t=32)[:, :, 0:n], func=AF.Copy),
                            reads=[PB[bank]], writes=[bBt2[0 if jg * 16 < JD else 1]])

            def scan(bi):
                t0, n, sq = blocks[bi]
                for (eng, ja, jb, bT1, bU, bBt, bS) in halves:
                    def TT(out, in0, in1, op, reads, writes, strict=None, eng=eng):
                        fw.op(eng, lambda e: e.tensor_tensor(out=out, in0=in0, in1=in1, op=op), reads=reads, writes=writes,
                              strict=strict)
                    for t in range(n):
                        prev = Scar[:, sq, :, ja:jb] if t == 0 else Bt[:, :, ja:jb, t - 1]
                        pbuf = [bS] if t == 0 else [bBt]
                        TT(T1[:, :, ja:jb], A1s[:, :, ja:jb], prev, ALU.mult, pbuf + [bConst], [bT1], t == 0)
                        TT(U_[:, :, ja:jb], A2s[:, :, ja:jb], prev, ALU.mult, pbuf + [bConst], [bU], t == 0)
                        TT(T1[:, 0, ja:jb], T1[:, 0, ja:jb], U_[:, 1, ja:jb], ALU.subtract, [bT1, bU], [bT1])
                        TT(T1[:, 1, ja:jb], T1[:, 1, ja:jb], U_[:, 0, ja:jb], ALU.add, [bT1, bU], [bT1])
                        TT(Bt[:, :, ja:jb, t], Bt[:, :, ja:jb, t], T1[:, :, ja:jb], ALU.add, [bBt, bT1], [bBt])
                    fw.op(eng, lambda e, sq=sq, n=n, ja=ja, jb=jb: e.tensor_copy(out=Scar[:, sq, :, ja:jb],
                                                                                 in_=Bt[:, :, ja:jb, n - 1]),
                          reads=[bBt], writes=[bS], strict=True)

            def cast(bi):
                t0, n, sq = blocks[bi]
                for r in range(2):
                    act(HbT[:, r, :, 0:n], Bt[:, r, :, 0:n], AF.Copy, bBt2, [bHb])

            def Cproj(bi):
                t0, n, sq = blocks[bi]
                bank = 4 + (bi % 2)
                for i in range(16):
                    oc = ps[bank][:, i * 32:i * 32 + n]
                    mm(oc, Dg[:, i, :], uB[:, i, t0:t0 + n], True, False, [bConst, UB[i]], [PB[bank]])
                    for hf in range(2):
                        for qq in range(2):
                            j = 4 * i + 2 * hf + qq
                            for r in range(2):
                                lastmm = (hf == 1 and qq == 1 and r == 1)
                                mm(ps[bank][64 * hf:64 * hf + 64, i * 32:i * 32 + n], CwT[:, r, j, :],
                                   HbT[:, r, j, 0:n], False, lastmm, [bTMP, bHb], [PB[bank]])
                fw.op("act", lambda e, bank=bank, t0=t0, n=n: e.activation(
                    out=hB[:, :, t0:t0 + n],
                    in_=ps[bank][:, :].rearrange("p (j t) -> p j t", t=32)[:, :, 0:n], func=AF.Gelu),
                    reads=[PB[bank]], writes=HBb)

            for bi in range(len(blocks)):
                Bproj(bi)
                scan(bi)
                if bi >= 1:
                    Cproj(bi - 1)
                cast(bi)
            Cproj(len(blocks) - 1)
            fw.barrier(engines=("pe", "dve", "pool"))
            ck("t_scan")

            def prod(m):
                wv, wb = wget("glu", m)
                pa, pg = (m % 2) * 2, (m % 2) * 2 + 1
                for kc in range(16):
                    mm(ps[pa][:, 0:T], wv[:, kc, 0, :], hB[:, kc, 0:T], kc == 0, kc == 15, [wb, HBb[kc]], [PB[pa]])
                for kc in range(16):
                    mm(ps[pg][:, 0:T], wv[:, kc, 1, :], hB[:, kc, 0:T], kc == 0, kc == 15, [wb, HBb[kc]], [PB[pg]])
                sg = TMP[:, (m % 2) * 512:(m % 2) * 512 + 512]
                mx = TMP[:, 1024 + (m % 2) * 512:1024 + (m % 2) * 512 + 512]
                bsg, bmx = B(f"sg{m % 2}"), B(f"mx{m % 2}")
                act(sg[:, 0:T], ps[pg][:, 0:T], AF.Sigmoid, [PB[pg]], [bsg])
                tt(mx[:, 0:T], ps[pa][:, 0:T], sg[:, 0:T], ALU.mult, [PB[pa], bsg], [bmx])
                return mx, [bmx]

            ln_block(0, 0, T, segs, prod, G1[:, 0], False)
            fw.barrier()
            ck("t_ssm")

        def attn_full(kind, ti, T, segs):
            QT = [B(f"q{c}") for c in range(DC)]
            bKf = B("kTf")
            for b in range(16):
                wv, wb = wget("qkv", b)
                for oc in range(2):
                    m = (2 * b + oc) % 16
                    pb = (2 * b + oc) % 4
                    for kc in range(16):
                        mm(ps[pb][:, 0:T], wv[:, kc, oc * 128:(oc + 1) * 128], hB[:, kc, 0:T], kc == 0, kc == 15,
                           [wb, HBb[kc]], [PB[pb]])
                    if b < 8:
                        act(qT[:, m, 0:T], ps[pb][:, 0:T], AF.Copy, [PB[pb]], [QT[m]], scale=128.0 ** -0.5)
                    else:
                        act(kTf[:, m, 0:T], ps[pb][:, 0:T], AF.Copy, [PB[pb]], [bKf])
            kdst = k_p[ti * TT:(ti + 1) * TT, :] if kind == "p" else k_s
            store_rows(kdst, kTf, [bKf] * 16, T)
            fw.barrier()
            ck("t_qk")
            kTb16 = TMP[:, 4096:6144].bitcast(BF16).rearrange("p (c t) -> p c t", t=256)
            bk16 = B("kTb16")
            knew = TMP[:, 4096:5120].bitcast(BF16).rearrange("p (c t) -> p c t", t=128)
            if kind == "p":
                for hh in range(2):
                    for c in range(16):
                        act(kTb16[:, c, :], kTf[:, c, hh * 256:(hh + 1) * 256], AF.Copy, [bKf], [bk16])
                    sp_dma(kT_scr[:, :, ti * TT + hh * 256:ti * TT + (hh + 1) * 256].rearrange("s d t -> d s t"),
                           kTb16, [bk16], [bKscr], key=bk16)
            else:
                for c in range(16):
                    act(knew[:, c, 0:64], kTf[:, c, 0:64], AF.Copy, [bKf], [bk16])
            vnew = Rr[0:32, 4096:8192].bitcast(BF16).rearrange("p (q e) -> p q e", q=2)
            bvn = B("vnew")
            vs32 = [TMP[:, 0:256], TMP[:, 256:512], TMP[:, 512:768], TMP[:, 768:1024]]
            vs16 = [TMP[:, 1024:1152].bitcast(BF16), TMP[:, 1152:1280].bitcast(BF16),
                    TMP[:, 1280:1408].bitcast(BF16), TMP[:, 1408:1536].bitcast(BF16)]
            if kind == "p":
                units = [(tc, 128, tc * 128) for tc in range(T // 128)]
            else:
                units = [(q, 32, q * 32) for q in range(2)]
            cnt = 0
            for b in range(8):
                wv, wb = wget("qkv", 16 + b)
                for (ui, rows, c0) in units:
                    pb = cnt % 4
                    sidx = cnt % 4
                    cnt += 1
                    for kc in range(16):
                        mm(ps[pb][0:rows, 0:256], hB[:, kc, c0:c0 + rows], wv[:, kc, :], kc == 0, kc == 15,
                           [wb, HBb[kc]], [PB[pb]])
                    b32, b16 = B(f"vs32_{sidx}"), B(f"vs16_{sidx}")
                    act(vs32[sidx][0:rows, :], ps[pb][0:rows, 0:256], AF.Copy, [PB[pb]], [b32])
                    if kind == "p":
                        act(vs16[sidx][0:rows, :], ps[pb][0:rows, 0:256], AF.Copy, [PB[pb]], [b16])
                        r0 = ti * TT + c0
                        sp_dma(v_p[r0:r0 + rows, b * 256:(b + 1) * 256], vs32[sidx][0:rows, :], [b32], [], key=b32,
                               is_output=True)
                        sp_dma(v_scr[r0:r0 + rows, b * 256:(b + 1) * 256], vs16[sidx][0:rows, :], [b16], [bVscr], key=b16)
                    else:
                        act(vnew[0:rows, ui, b * 256:(b + 1) * 256], ps[pb][0:rows, 0:256], AF.Copy, [PB[pb]], [bvn])
                        sp_dma(v_s[c0:c0 + rows, b * 256:(b + 1) * 256], vs32[sidx][0:rows, :], [b32], [], key=b32,
                               is_output=True)
            fw.barrier()
            ck("t_v")

            Pt = [TMP[:, 0:256].bitcast(BF16), TMP[:, 256:512].bitcast(BF16), TMP[:, 512:768].bitcast(BF16)]
            bPt = [B("Pt0"), B("Pt1"), B("Pt2")]
            On = [TMP[:, 768:1792].rearrange("p (a t) -> p a t", a=2), TMP[:, 1792:2816].rearrange("p (a t) -> p a t", a=2)]
            bOn = [B("On0"), B("On1")]
            rec = TMP[:, 2816:3328]
            brec = B("rec")
            osq = TMP[:, 3328:3840].bitcast(BF16).rearrange("p (a t) -> p a t", a=2)
            bosq = B("osq")
            rr = TMP[:, 2816:3328]
            brr = B("rr")
            kst = TMP[:, 5120:6144].rearrange("p (k e) -> p k e", e=256)
            bkst = B("kst")
            KT1 = Rr[:, 0:4096].bitcast(BF16).rearrange("p (s t) -> p s t", s=2)
            Vh1 = Rr[:, 4096:8192].bitcast(BF16).rearrange("p (k e) -> p k e", e=256)
            KTs, Vhs = [KTt, KT1], [Vht, Vh1]
            bKTs, bVhs = [B("KTt"), B("KT1")], [B("Vht"), B("Vh1")]
            if kind == "p":
                qsets = [(0, T, None)]
            else:
                qsets = [(0, 32, 0), (32, 64, 1)]
            pend = []
            for (q0, q1, sq) in qsets:
                nq = q1 - q0
                for h in range(NH):
                    bs_ = (h % 2) if kind == "p" else 0
                    KTc, Vhc, bKT, bVh = KTs[bs_], Vhs[bs_], bKTs[bs_], bVhs[bs_]
                    chunks = []
                    if kind == "p":
                        kend = (ti + 1) * TT
                        sp_dma(KTc[:, :, 0:kend], kT_scr[2 * h:2 * h + 2, :, 0:kend].rearrange("s d t -> d s t"),
                               [bKscr], [bKT], key=bKT)
                        sp_dma(Vhc[:, 0:kend // 128, :],
                               v_scr[0:kend, h * 256:(h + 1) * 256].rearrange("(k p) e -> p k e", p=128),
                               [bVscr], [bVh], key=bVh)
                        for kc in range(kend // 128):
                            dg = kc - 4 * ti
                            c_lo = 0 if dg < 0 else 128 * dg
                            chunks.append((kc * 128, 128, (lambda ec, kc=kc, Vhc=Vhc: Vhc[:, kc, ec * 128:(ec + 1) * 128]), c_lo,
                                           dg >= 0, [bVh]))
                    else:
                        kstK = TMP[:, 5120:5632].rearrange("p (k e) -> p k e", e=256)
                        kstV = TMP[:, 5632:6144].rearrange("p (k e) -> p k e", e=256)
                        bkK, bkV = B("kstK"), B("kstV")
                        for g8 in range(8):
                            sp_dma(kstK, cache_k[sq, g8 * 256:(g8 + 1) * 256, 2 * h * 128:(2 * h + 2) * 128]
                                   .rearrange("(k p) e -> p k e", p=128), [], [bkK], key=bkK)
                            sp_dma(kstV, cache_v[sq, g8 * 256:(g8 + 1) * 256, h * 256:(h + 1) * 256]
                                   .rearrange("(k p) e -> p k e", p=128), [], [bkV], key=bkV)
                            for s2 in range(2):
                                pb = tps()
                                for k2 in range(2):
                                    tp(ps[pb][:, k2 * 128:(k2 + 1) * 128], kstK[:, k2, s2 * 128:(s2 + 1) * 128], 128,
                                       [bkK], [PB[pb]])
                                act(KTc[:, s2, g8 * 256:(g8 + 1) * 256], ps[pb][:, 0:256], AF.Copy, [PB[pb]], [bKT])
                            act(Vhc[:, g8 * 2:(g8 + 1) * 2, :], kstV, AF.Copy, [bkV], [bVh])
                        for s2 in range(2):
                            act(KTc[:, s2, PAST:PAST + 32], knew[:, 2 * h + s2, q0:q1], AF.Copy, [bk16], [bKT])
                        for kc in range(16):
                            chunks.append((kc * 128, 128, (lambda ec, kc=kc, Vhc=Vhc: Vhc[:, kc, ec * 128:(ec + 1) * 128]), 0,
                                           False, [bVh]))
                        chunks.append((PAST, 32, (lambda ec, sq=sq: vnew[0:32, sq, h * 256 + ec * 128:h * 256 + (ec + 1) * 128]),
                                       0, False, [bvn]))
                    for s2 in range(2):
                        sub = 2 * h + s2
                        nck = len(chunks)
                        ab = 2 + 3 * s2

                        def score(ci):
                            koff, nk, vsrc, c_lo, diag, vb = chunks[ci]
                            sb_ = ci % 2
                            pt, bpt = Pt[ci % 3], bPt[ci % 3]
                            mm(ps[sb_][0:nk, c_lo:nq], KTc[:, s2, koff:koff + nk], qT[:, sub, q0 + c_lo:q1], True, True,
                               [bKT, QT[sub]], [PB[sb_]])
                            act(pt[0:nk, c_lo:nq], ps[sb_][0:nk, c_lo:nq], AF.Exp, [PB[sb_]], [bpt])
                            if diag:
                                dve(lambda e, pt=pt, c_lo=c_lo: e.memset(pt[64:128, c_lo:c_lo + 64], 0.0), [bpt], [bpt])

                        def pv(ci):
                            koff, nk, vsrc, c_lo, diag, vb = chunks[ci]
                            pt, bpt = Pt[ci % 3], bPt[ci % 3]
                            for ec in range(2):
                                mm(ps[ab + ec][:, c_lo:nq], vsrc(ec), pt[0:nk, c_lo:nq], ci == 0, ci == nck - 1,
                                   vb + [bpt], [PB[ab + ec]])
                            mm(ps[ab + 2][:, c_lo:nq], ones_b[0:nk, :], pt[0:nk, c_lo:nq], ci == 0, ci == nck - 1,
                               [bConst, bpt], [PB[ab + 2]])

                        for ci in range(nck + 1):
                            if ci < nck:
                                score(ci)
                            if ci >= 1:
                                pv(ci - 1)
                        if s2 == 0 and pend:
                            pend.pop()()
                        dve(lambda e, nq=nq, ab=ab: e.reciprocal(out=rec[:, 0:nq], in_=ps[ab + 2][:, 0:nq]), [PB[ab + 2]],
                            [brec], True)
                        for ec in range(2):
                            tt(On[s2][:, ec, 0:nq], ps[ab + ec][:, 0:nq], rec[:, 0:nq], ALU.mult, [PB[ab + ec], brec],
                               [bOn[s2]], True)
                    for ec in range(2):
                        stt(On[0][:, ec, 0:nq], On[1][:, ec, 0:nq], lamt[:, 2:3], On[0][:, ec, 0:nq], ALU.mult, ALU.add,
                            [bOn[1], bOn[0], bConst], [bOn[0]], True)
                        act(osq[:, ec, 0:nq], On[0][:, ec, 0:nq], AF.Square, [bOn[0]], [bosq])
                    def epilogue(h=h, q0=q0, q1=q1, nq=nq):
                        for ec in range(2):
                            mm(ps[7][:, 0:nq], ones_b[:], osq[:, ec, 0:nq], ec == 0, ec == 1, [bosq, bConst], [PB[7]])
                        ts(rr[:, 0:nq], ps[7][:, 0:nq], 1.0 / 256.0, LN_EPS, ALU.mult, ALU.add, [PB[7], brec], [brr, brec], True)
                        act(rr[:, 0:nq], rr[:, 0:nq], AF.Sqrt, [brr], [brr])
                        dve(lambda e: e.reciprocal(out=rr[:, 0:nq], in_=rr[:, 0:nq]), [brr], [brr], True)
                        for ec in range(2):
                            stt(hB[:, 2 * h + ec, q0:q1], On[0][:, ec, 0:nq], sgs[:, ec:ec + 1], rr[:, 0:nq], ALU.mult,
                                ALU.mult, [bOn[0], brr, bConst], [HBb[2 * h + ec]], True)

                    pend.append(epilogue)
            if pend:
                pend.pop()()
            fw.barrier()
            ck("t_attn")

            wcache = {}

            def prod(m):
                b = m // 2
                if b not in wcache:
                    wcache.clear()
                    wcache[b] = wget("o", b)
                wv, wb = wcache[b]
                pb = m % 4
                for kc in range(16):
                    mm(ps[pb][:, 0:T], wv[:, kc, (m % 2) * 128:(m % 2) * 128 + 128], hB[:, kc, 0:T], kc == 0, kc == 15,
                       [wb, HBb[kc]], [PB[pb]])
                return ps[pb], [PB[pb]]

            ln_block(1, 0, T, segs, prod, G1[:, 1], True)
            fw.barrier()
            ck("t_wo")

        for (kind, ti) in tiles:
            if kind == "p":
                T = TT
                segs = [(0, TT, 0)]
                blocks = [(32 * b, 32, 0) for b in range(TT // 32)]
                src_rows = x_p[ti * TT:(ti + 1) * TT, :]
                dst_rows = y_p[ti * TT:(ti + 1) * TT, :]
            else:
                T = 64
                segs = [(0, 32, 1), (32, 64, 2)]
                blocks = [(0, 32, 1), (32, 32, 2)]
                src_rows = x_s
                dst_rows = y_s
            load_x(src_rows, T)
            for m in range(DC):
                for (c0, c1, s) in segs:
                    act(hB[:, m, c0:c1], xT[:, m, c0:c1], AF.Identity, [XT[m], bConst], [HBb[m]],
                        bias=modt[:, 0, m, s:s + 1], scale=A1m[:, m, s:s + 1])
            fw.barrier()
            ck("t_load")
            def dbg(i):
                if debug:
                    store_rows(dbg_p[i, ti * TT:(ti + 1) * TT, :] if kind == "p" else dbg_s[i], xT, XT, T)
                    fw.barrier()
                    if i == 0:
                        for c in range(16):
                            act(kTf[:, c, 0:T], hB[:, c, 0:T], AF.Copy, [HBb[c]], [bR])
                        store_rows(dbg_p[3, ti * TT:(ti + 1) * TT, :] if kind == "p" else dbg_s[3], kTf, [bR] * 16, T)
                        fw.barrier()
            ssm_layer(T, segs, blocks)
            dbg(0)
            ffn(0, T, segs, False)
            dbg(1)
            attn_full(kind, ti, T, segs)
            dbg(2)
            ffn(1, T, segs, True)
            store_rows(dst_rows, xT, XT, T)
            fw.barrier()
            ck("t_tile")

        store_T(sre_p, Scar[:, 0, 0, :], 64, [B("Scar")])
        store_T(sim_p, Scar[:, 0, 1, :], 64, [B("Scar")])
        for l in range(2):
            for t2 in range(2):
                store_T(conv_p[l, t2].rearrange("(k f) -> k f", f=128), ccar[:, l, 0, :, t2], FC, [bConst])
        if do_sample:
            for q in range(2):
                store_T(sre_s[q], Scar[:, 1 + q, 0, :], 64, [B("Scar")])
                store_T(sim_s[q], Scar[:, 1 + q, 1, :], 64, [B("Scar")])
                for l in range(2):
                    for t2 in range(2):
                        store_T(conv_s[l, q, t2].rearrange("(k f) -> k f", f=128), ccar[:, l, 1 + q, :, t2], FC, [bConst])
        fw.finish()
        fw.emit()
    return nc


def make_in_maps(inp, n_cores=8, n_tiles=8):
    f = lambda a: np.ascontiguousarray(np.asarray(a, dtype=np.float32))
    SEQ = TT * n_tiles
    shared = {
        "w_ada": f(inp["w_ada"]), "b_ada": f(inp["b_ada"]).reshape(2, 96, 128),
        "ln_g": f(inp["ln_g"]).reshape(64, 128), "ln_b": f(inp["ln_b"]).reshape(64, 128),
        "w_up": f(inp["w_up"]), "w_dconv": f(inp["w_dconv"]).reshape(264, 128),
        "b_dconv": f(inp["b_dconv"]).reshape(88, 128), "w_down": f(inp["w_down"]),
        "w_in": f(inp["w_ssm_in"][0]), "lam_re": f(inp["ssm_lam_re"][0]), "lam_im": f(inp["ssm_lam_im"][0]),
        "log_step": f(inp["ssm_log_step"][0]), "b_re": f(inp["ssm_b_re"][0]).reshape(128, 1024),
        "b_im": f(inp["ssm_b_im"][0]).reshape(128, 1024), "c_re": f(inp["ssm_c_re"][0]).reshape(2048, 64),
        "c_im": f(inp["ssm_c_im"][0]).reshape(2048, 64), "ssm_d": f(inp["ssm_d"][0]).reshape(16, 128),
        "w_glu": f(inp["w_glu"][0]), "w_qkv": f(inp["w_qkv"][0]),
        "lamv": f(np.stack([inp["lam_q1"][0], inp["lam_k1"][0], inp["lam_q2"][0], inp["lam_k2"][0]])),
        "subln": f(inp["subln_g"][0]).reshape(2, 128), "w_o": f(inp["w_o"][0]),
    }
    maps = []
    for c in range(n_cores):
        m = dict(shared)
        m["x_p"] = f(inp["x_prompt"][c, :SEQ])
        m["x_s"] = f(inp["x_sample"][2 * c:2 * c + 2]).reshape(64, D)
        m["c_all"] = f(np.concatenate([inp["c_prompt"][c:c + 1], inp["c_sample"][2 * c:2 * c + 2]], axis=0))
        m["cache_k"] = f(inp["cache_k"][0, 2 * c:2 * c + 2]).reshape(2, PAST, D)
        m["cache_v"] = f(inp["cache_v"][0, 2 * c:2 * c + 2]).reshape(2, PAST, D)
        m["st_re"] = f(inp["state_ssm_re"][0, 2 * c:2 * c + 2]).reshape(2, 64, 128)
        m["st_im"] = f(inp["state_ssm_im"][0, 2 * c:2 * c + 2]).reshape(2, 64, 128)
        m["st_conv"] = f(np.asarray(inp["state_conv"])[:, 2 * c:2 * c + 2])
        maps.append(m)
    return maps


def assemble(results, n_cores=8, n_tiles=8):
    SEQ = TT * n_tiles
    g = lambda name: [np.asarray(r[name], dtype=np.float32) for r in results]
    y_p = np.stack(g("y_p")).reshape(n_cores, SEQ, D)
    y_s = np.concatenate([a.reshape(2, 32, D) for a in g("y_s")], axis=0)
    k_p = np.stack(g("k_p")).reshape(1, n_cores, SEQ, 16, 128)
    v_p = np.stack(g("v_p")).reshape(1, n_cores, SEQ, 8, 256)
    sre_p = np.stack(g("sre_p")).reshape(1, n_cores, 128, 64)
    sim_p = np.stack(g("sim_p")).reshape(1, n_cores, 128, 64)
    conv_p = np.stack(g("conv_p"), axis=1).reshape(2, n_cores, 2, DFF)
    k_s = np.concatenate([a.reshape(2, 32, 16, 128) for a in g("k_s")], axis=0)[None]
    v_s = np.concatenate([a.reshape(2, 32, 8, 256) for a in g("v_s")], axis=0)[None]
    sre_s = np.concatenate([a.reshape(2, 128, 64) for a in g("sre_s")], axis=0)[None]
    sim_s = np.concatenate([a.reshape(2, 128, 64) for a in g("sim_s")], axis=0)[None]
    conv_s = np.concatenate(g("conv_s"), axis=1)
    return (y_p, y_s, k_p, v_p, sre_p, sim_p, conv_p, k_s, v_s, sre_s, sim_s, conv_s)


_NC_CACHE = {}


def kernel(**inputs):
    if "nc" not in _NC_CACHE:
        _NC_CACHE["nc"] = build()
    nc = _NC_CACHE["nc"]
    maps = make_in_maps(inputs)
    res = run_bass_kernel_spmd(nc, maps, core_ids=list(range(8)))
    return assemble(res.results)
```

```python
import math
from contextlib import ExitStack
import numpy as np
import concourse.bass as bass
import concourse.mybir as mybir
from concourse.bass_utils import run_bass_kernel_spmd

F32 = mybir.dt.float32
BF16 = mybir.dt.bfloat16
I32 = mybir.dt.int32
AF = mybir.ActivationFunctionType
ALU = mybir.AluOpType
ROT = 12000

D = 2048
DC = 16
DFF = 5632
FC = 44
TT = 512
NH = 8
PAST = 2048
ALPHA = 4.0 ** 0.25
LN_EPS = 1e-5
EPS_LN = LN_EPS / (ALPHA * ALPHA)
LAM_INIT = 0.8 - 0.6 * math.exp(-0.3 * 1)


class _Stop(Exception):
    pass


class Buf:
    __slots__ = ("name", "lw", "rd", "sem", "cnt")

    def __init__(self, name):
        self.name = name
        self.lw = None
        self.rd = []
        self.sem = None
        self.cnt = 0


class FW:
    ENGS = ("pe", "act", "dve", "pool", "sp")

    def __init__(self, nc, stack):
        self.nc = nc
        self.stack = stack
        self.prog = {e: [] for e in self.ENGS}
        self.count = {e: 0 for e in self.ENGS}
        self.waited = {e: {} for e in self.ENGS}
        self.force = {e: [] for e in self.ENGS}
        self.sems = {}
        self.out_tokens = []
        self.pending = []
        self.fence = Buf("fence")
        self.dummy = None
        self.stopped = False
        self.strict = False
        self.strict_default = False

    def _sem(self, key):
        s = self.sems.get(key)
        if s is None:
            s = self.stack.enter_context(self.nc.semaphore("s_" + "_".join(str(k) for k in key)))
            self.sems[key] = s
        return s

    def _tok_eng(self, eng):
        n = self.count[eng]
        return (("e", eng, (n - 1) // ROT), (n - 1) % ROT + 1)

    def _need(self, eng, tok, waits):
        if tok is None:
            return
        key, val = tok
        if key[0] == "e" and key[1] == eng and not self.strict:
            return
        w = self.waited[eng]
        if w.get(key, 0) >= val:
            return
        w[key] = val
        waits.append((key, val))

    def _deps(self, eng, reads, writes):
        waits = []
        for t in self.force[eng]:
            self._need(eng, t, waits)
        self.force[eng] = []
        for b in reads:
            self._need(eng, b.lw, waits)
        for b in writes:
            self._need(eng, b.lw, waits)
            for t in b.rd:
                self._need(eng, t, waits)
        return waits

    def op(self, eng, fn, reads=(), writes=(), strict=None):
        if self.stopped:
            return None
        self.strict = self.strict_default if strict is None else strict
        waits = self._deps(eng, reads, writes)
        self.strict = False
        self.count[eng] += 1
        tok = self._tok_eng(eng)
        self._sem(tok[0])
        for b in writes:
            b.lw = tok
            b.rd = []
        for b in reads:
            b.rd = [t for t in b.rd if t[0] != tok[0]] + [tok]
        self.prog[eng].append((waits, fn, tok))
        return tok

    def dma(self, eng, fn, reads=(), writes=(), key=None, is_output=False, ring=False):
        if self.stopped:
            return None
        kb = key
        if kb.sem is None:
            kb.sem = ("d", kb.name)
            self._sem(kb.sem)
        saved = []
        for b in writes:
            if b.lw is not None and b.lw[0] == kb.sem and not b.rd:
                saved.append((b, b.lw))
                b.lw = None
        waits = self._deps(eng, reads, writes)
        for b, t in saved:
            b.lw = t
        kb.cnt += 16
        tok = (kb.sem, kb.cnt)
        for b in writes:
            b.lw = tok
            b.rd = []
        for b in reads:
            b.rd = b.rd + [tok]
        self.prog[eng].append((waits, fn, tok))
        if is_output:
            self.out_tokens.append(tok)
        if not ring:
            self.pending.append(tok)
        return tok

    def barrier(self, engines=("pe", "dve")):
        if self.stopped:
            return
        waits = self._deps("act", (), ())
        for eng in engines:
            if self.count[eng] > 0:
                self._need("act", self._tok_eng(eng), waits)
        for tok in self.pending:
            self._need("act", tok, waits)
        self.pending = []
        self.count["act"] += 1
        tok = self._tok_eng("act")
        self._sem(tok[0])
        d = self.dummy
        self.prog["act"].append((waits, lambda e: e.activation(out=d[:, 0:1], in_=d[:, 1:2], func=AF.Copy), tok))
        for eng in engines:
            self.force[eng].append(tok)
        self.fence.lw = tok
        self.fence.rd = []

    def finish(self, eng="sp"):
        waits = []
        for tok in self.out_tokens:
            self._need(eng, tok, waits)
        for tok in self.pending:
            self._need(eng, tok, waits)
        self.prog[eng].append((waits, None, None))

    def emit(self):
        nc = self.nc
        hmap = {"pe": "tensor", "act": "scalar", "dve": "vector", "pool": "gpsimd", "sp": "sync"}
        with nc.Block() as block:
            for eng in self.ENGS:
                prog = self.prog[eng]
                if not prog:
                    continue

                def body(e, prog=prog):
                    for waits, fn, tok in prog:
                        for key, val in waits:
                            e.wait_ge(self.sems[key], val)
                        if fn is None:
                            continue
                        inst = fn(e)
                        inst.then_inc(self.sems[tok[0]], 1 if tok[0][0] == "e" else 16)

                getattr(block, hmap[eng])(body)


def build(n_tiles=8, do_sample=True, stop_after=None, debug=False):
    nc = bass.Bass("TRN2", target_bir_lowering=False)
    SEQ = TT * n_tiles

    def din(name, shape, dtype=F32):
        return nc.dram_tensor(name, list(shape), dtype, kind="ExternalInput").ap()

    def dout(name, shape, dtype=F32):
        return nc.dram_tensor(name, list(shape), dtype, kind="ExternalOutput").ap()

    def dscr(name, shape, dtype):
        return nc.dram_tensor(name, list(shape), dtype, kind="Internal").ap()

    x_p = din("x_p", [SEQ, D]); x_s = din("x_s", [64, D]); c_all = din("c_all", [3, D])
    cache_k = din("cache_k", [2, PAST, D]); cache_v = din("cache_v", [2, PAST, D])
    st_re = din("st_re", [2, 64, 128]); st_im = din("st_im", [2, 64, 128])
    st_conv = din("st_conv", [2, 2, 2, DFF])
    w_ada = din("w_ada", [2, D, 6 * D]); b_ada = din("b_ada", [2, 96, 128])
    ln_g = din("ln_g", [64, 128]); ln_b = din("ln_b", [64, 128])
    w_up = din("w_up", [2, D, 2 * DFF]); w_dconv = din("w_dconv", [264, 128]); b_dconv = din("b_dconv", [88, 128])
    w_down = din("w_down", [2, DFF, D]); w_in = din("w_in", [D, D])
    lam_re = din("lam_re", [128, 64]); lam_im = din("lam_im", [128, 64]); log_step = din("log_step", [128])
    b_re = din("b_re", [128, 1024]); b_im = din("b_im", [128, 1024])
    c_re = din("c_re", [2048, 64]); c_im = din("c_im", [2048, 64]); ssm_d = din("ssm_d", [16, 128])
    w_glu = din("w_glu", [D, 2 * D]); w_qkv = din("w_qkv", [D, 3 * D]); lamv = din("lamv", [4, 128])
    subln = din("subln", [2, 128]); w_o = din("w_o", [D, D])

    y_p = dout("y_p", [SEQ, D]); y_s = dout("y_s", [64, D])
    k_p = dout("k_p", [SEQ, D]); v_p = dout("v_p", [SEQ, D])
    sre_p = dout("sre_p", [64, 128]); sim_p = dout("sim_p", [64, 128]); conv_p = dout("conv_p", [2, 2, DFF])
    k_s = dout("k_s", [64, D]); v_s = dout("v_s", [64, D])
    sre_s = dout("sre_s", [2, 64, 128]); sim_s = dout("sim_s", [2, 64, 128]); conv_s = dout("conv_s", [2, 2, 2, DFF])

    if debug:
        dbg_p = dout("dbg_p", [4, SEQ, D]); dbg_s = dout("dbg_s", [4, 64, D])
    kT_scr = dscr("kT_scr", [16, 128, SEQ], BF16)
    v_scr = dscr("v_scr", [SEQ, D], BF16)
    bb_scr = dscr("bb_scr", [2, 2048, 64], F32)
    bw_scr = dscr("bw_scr", [128, 2 * 64 * 128], BF16)
    cw_scr = dscr("cw_scr", [128, 2 * 64 * 64], BF16)
    wscr = dscr("wscr", [176, 128, 5632], BF16)

    st = ExitStack()
    with st:
        fw = FW(nc, st)

        def sb(name, shape, dtype):
            return st.enter_context(nc.sbuf_tensor(name, list(shape), dtype))

        def ck(name):
            if stop_after == name:
                fw.stopped = True

        xT = sb("xT", [128, DC, TT], F32)
        hB = sb("hB", [128, DC, TT], BF16)
        Rr = sb("Rr", [128, 8192], F32)
        BIG = sb("BIG", [128, 24576], BF16)
        TMP = sb("TMP", [128, 6144], F32)
        wr = [sb(f"wr{i}", [128, 5632], BF16) for i in range(3)]
        ident = sb("ident", [128, 128], F32)
        ones_b = sb("ones_b", [128, 128], BF16)
        Dg = sb("Dg", [128, 16, 128], BF16)
        dummy = sb("dummyt", [128, 2], F32)
        modt = sb("modt", [128, 2, 96, 3], F32)
        A1m = sb("A1m", [128, 16, 3], F32)
        G1 = sb("G1", [128, 2, 16, 3], F32)
        G2 = sb("G2", [128, 2, 16, 3], F32)
        HS2 = sb("HS2", [128, 2, 16, 3], F32); HB2 = sb("HB2", [128, 2, 16, 3], F32)
        HS1n = sb("HS1n", [128, 16, 3], F32); HB1n = sb("HB1n", [128, 16, 3], F32)
        lng = sb("lng", [128, 64], F32); lnb = sb("lnb", [128, 64], F32)
        bada = sb("bada", [128, 2, 96], F32)
        wdc = sb("wdc", [128, 264], F32); bdc = sb("bdc", [128, 88], F32)
        dsk = sb("dsk", [128, 16], F32)
        ccar = sb("ccar", [128, 2, 3, FC, 2], F32)
        A1s = sb("A1s", [128, 2, 64], F32); A2s = sb("A2s", [128, 2, 64], F32)
        Scar = sb("Scar", [128, 3, 2, 64], F32)
        lamt = sb("lamt", [128, 4], F32)
        sgs = sb("sgs", [128, 2], F32)
        cTb = sb("cTb", [128, 16, 3], BF16)
        ps = [st.enter_context(nc.psum_tensor(f"ps{i}", [128, 512], F32)) for i in range(8)]
        fw.dummy = dummy

        PB = [Buf(f"ps{i}") for i in range(8)]
        XT = [Buf(f"xT{c}") for c in range(DC)]
        HBb = [Buf(f"hB{c}") for c in range(DC)]
        WB = [Buf(f"wr{i}") for i in range(3)]
        bR = Buf("R"); bTMP = Buf("TMP"); bBIG = Buf("BIG")
        bConst = Buf("const")
        bKscr = Buf("kscr"); bVscr = Buf("vscr")
        bufs = {}

        def B(name):
            b = bufs.get(name)
            if b is None:
                b = Buf(name)
                bufs[name] = b
            return b

        def bigv(off_bf16, n):
            return BIG[:, off_bf16:off_bf16 + n]

        actT = BIG[:, 0:FC * TT].rearrange("p (c t) -> p c t", t=TT)
        uB = BIG[:, 0:8192].rearrange("p (c t) -> p c t", t=TT)
        HbT = Rr[:, 4096:6144].bitcast(BF16).rearrange("p (r j t) -> p r j t", r=2, j=64)
        CwT = TMP[:, 0:4096].bitcast(BF16).rearrange("p (r j k) -> p r j k", r=2, j=64)
        CwI = BIG[:, 0:8192].rearrange("p (r j k) -> p r j k", r=2, j=64)
        KTt = BIG[:, 0:8192].rearrange("p (s t) -> p s t", s=2)
        Vht = BIG[:, 8192:16384].rearrange("p (k e) -> p k e", e=256)
        qT = BIG[:, 16384:24576].rearrange("p (c t) -> p c t", t=TT)
        Bt = Rr[:, 0:4096].rearrange("p (r j t) -> p r j t", r=2, j=64)
        kTf = Rr[:, 0:8192].rearrange("p (c t) -> p c t", t=TT)
        BwT = BIG[:, 8192:24576].rearrange("p (r j k) -> p r j k", r=2, j=64)
        stg = [TMP[:, 0:2048], TMP[:, 2048:4096]]
        bStg = [B("stg0"), B("stg1")]

        def mm(out, lhsT, rhs, start, stop, reads, writes):
            fw.op("pe", lambda e: e.matmul(out, lhsT=lhsT, rhs=rhs, start=start, stop=stop), reads=reads, writes=writes)

        def tp(out, in_, rows, reads, writes):
            fw.op("pe", lambda e: e.transpose(out=out, in_=in_, identity=ident[0:rows, 0:rows]),
                  reads=list(reads) + [bConst], writes=writes)

        def act(out, in_, func, reads, writes, bias=None, scale=None, strict=None):
            kw = {}
            if bias is not None:
                kw["bias"] = bias
            if scale is not None:
                kw["scale"] = scale
            fw.op("act", lambda e: e.activation(out=out, in_=in_, func=func, **kw), reads=reads, writes=writes,
                  strict=strict)

        def dve(fn, reads, writes, strict=None):
            fw.op("dve", fn, reads=reads, writes=writes, strict=strict)

        def tt(out, in0, in1, op, reads, writes, strict=None):
            dve(lambda e: e.tensor_tensor(out=out, in0=in0, in1=in1, op=op), reads, writes, strict)

        def ts(out, in0, s1, s2, op0, op1, reads, writes, strict=None):
            dve(lambda e: e.tensor_scalar(out=out, in0=in0, scalar1=s1, scalar2=s2, op0=op0, op1=op1), reads, writes,
                strict)

        def stt(out, in0, scalar, in1, op0, op1, reads, writes, strict=None):
            dve(lambda e: e.scalar_tensor_tensor(out=out, in0=in0, scalar=scalar, in1=in1, op0=op0, op1=op1),
                reads, writes, strict)

        def sp_dma(out, in_, reads, writes, key, is_output=False, nc_ok=False):
            fw.dma("sp", lambda e: e.dma_start(out=out, in_=in_, allow_slow_non_contiguous=nc_ok),
                   reads=list(reads) + [fw.fence], writes=writes, key=key, is_output=is_output)

        pst = [0]

        def tps():
            pst[0] ^= 1
            return 6 + pst[0]

        def load_T(dst, src, rows):
            k = tps()
            sp_dma(stg[k - 6][0:rows, 0:128], src, [], [bStg[k - 6]], key=bStg[k - 6])
            tp(ps[k][:, 0:rows], stg[k - 6][0:rows, 0:128], rows, [bStg[k - 6]], [PB[k]])
            act(dst, ps[k][:, 0:rows], AF.Copy, [PB[k]], [bConst])

        def store_T(dst, src, rows, rbufs, is_output=True):
            k = tps()
            tp(ps[k][0:rows, 0:128], src, 128, rbufs, [PB[k]])
            act(stg[k - 6][0:rows, 0:128], ps[k][0:rows, 0:128], AF.Copy, [PB[k]], [bStg[k - 6]])
            sp_dma(dst, stg[k - 6][0:rows, 0:128], [bStg[k - 6]], [], key=bStg[k - 6], is_output=is_output)

        wseq = []
        wpos = [0, 0]

        def wsrc(kind, a):
            if kind == "ada":
                l, b = a
                return w_ada[l].rearrange("(kc p) n -> p kc n", p=128)[:, :, b * 256:(b + 1) * 256], (16, 256)
            if kind == "in":
                return w_in.rearrange("(kc p) n -> p kc n", p=128)[:, :, a * 256:(a + 1) * 256], (16, 256)
            if kind == "glu":
                return (w_glu.rearrange("(kc p) (two n) -> p kc two n", p=128, two=2)[:, :, :, a * 128:(a + 1) * 128],
                        (16, 2, 128))
            if kind == "up":
                l, f = a
                return (w_up[l].rearrange("(kc p) (two n) -> p kc two n", p=128, two=2)[:, :, :, f * 128:(f + 1) * 128],
                        (16, 2, 128))
            if kind == "down":
                l, m = a
                return w_down[l].rearrange("(kc p) n -> p kc n", p=128)[:, :, m * 128:(m + 1) * 128], (44, 128)
            if kind == "qkv":
                return w_qkv.rearrange("(kc p) n -> p kc n", p=128)[:, :, a * 256:(a + 1) * 256], (16, 256)
            if kind == "o":
                return w_o.rearrange("(kc p) n -> p kc n", p=128)[:, :, a * 256:(a + 1) * 256], (16, 256)
            raise ValueError(kind)

        def wview(slot, shp):
            n = int(np.prod(shp))
            v = wr[slot][:, 0:n]
            if len(shp) == 2:
                return v.rearrange("p (k n) -> p k n", n=shp[1])
            return v.rearrange("p (k two n) -> p k two n", two=shp[1], n=shp[2])

        WS = [Buf(f"ws{i}") for i in range(3)]
        Wscr = [Buf(f"wscr{i}") for i in range(176)]

        def w_issue():
            i = wpos[1]
            if i >= len(wseq):
                return
            kind, a = wseq[i]
            src, shp = wsrc(kind, a)
            slot = i % 3
            dstv = wview(slot, shp)
            n = int(np.prod(shp))
            tn, bi = ((i - 96) // 176, (i - 96) % 176) if kind != "ada" else (0, -1)
            if kind != "ada" and tn >= 1:
                fw.dma("pool", lambda e: e.dma_start(out=wr[slot][:, 0:n], in_=wscr[bi][:, 0:n]),
                       reads=[Wscr[bi]], writes=[WB[slot]], key=WB[slot], ring=True)
            else:
                if len(shp) == 2:
                    fw.dma("pool", lambda e: e.dma_start(out=dstv, in_=src), reads=[], writes=[WB[slot]], key=WB[slot],
                           ring=True)
                else:
                    for two in range(2):
                        fw.dma("pool", lambda e, two=two: e.dma_start(out=dstv[:, :, two, :], in_=src[:, :, two, :]),
                               reads=[], writes=[WB[slot]], key=WB[slot], ring=True)
                if kind != "ada" and len(tiles) > 1:
                    fw.dma("sp", lambda e: e.dma_start(out=wscr[bi][:, 0:n], in_=wr[slot][:, 0:n]),
                           reads=[WB[slot]], writes=[Wscr[bi]], key=WS[slot])
            wpos[1] += 1

        def wget(kind, a):
            i = wpos[0]
            assert wseq[i] == (kind, a), (wseq[i], kind, a)
            while wpos[1] < min(i + 3, len(wseq)):
                w_issue()
            wpos[0] += 1
            slot = i % 3
            _, shp = wsrc(kind, a)
            return wview(slot, shp), WB[slot]

        tiles = [("p", i) for i in range(n_tiles)] + ([("s", 0)] if do_sample else [])
        for l in range(2):
            for b in range(48):
                wseq.append(("ada", (l, b)))
        for _ in tiles:
            for b in range(8):
                wseq.append(("in", b))
            for m in range(16):
                wseq.append(("glu", m))
            for f in range(FC):
                wseq.append(("up", (0, f)))
            for m in range(16):
                wseq.append(("down", (0, m)))
            for b in range(24):
                wseq.append(("qkv", b))
            for b in range(8):
                wseq.append(("o", b))
            for f in range(FC):
                wseq.append(("up", (1, f)))
            for m in range(16):
                wseq.append(("down", (1, m)))

        fw.strict_default = True
        fw.op("pool", lambda e: e.memset(ident[:], 0.0), writes=[bConst])
        fw.op("pool", lambda e: e.affine_select(out=ident[:], in_=ident[:], pattern=[[-1, 128]],
                                                compare_op=ALU.not_equal, fill=1.0, base=0, channel_multiplier=1),
              reads=[bConst], writes=[bConst])
        fw.op("pool", lambda e: e.memset(ones_b[:], 1.0), writes=[bConst])
        fw.op("pool", lambda e: e.memset(dummy[:], 0.0), writes=[bConst])
        fw.op("pool", lambda e: e.memset(ccar[:], 0.0), writes=[bConst])
        fw.op("pool", lambda e: e.memset(Scar[:], 0.0), writes=[bConst])
        fw.barrier(engines=("pe", "dve", "pool"))

        load_T(lng[:, 0:64], ln_g, 64)
        load_T(lnb[:, 0:64], ln_b, 64)
        for l in range(2):
            load_T(bada[:, l, :], b_ada[l], 96)
        for i in range(3):
            load_T(wdc[:, i * 88:(i + 1) * 88], w_dconv[i * 88:(i + 1) * 88, :], 88)
        load_T(bdc[:, 0:88], b_dconv, 88)
        load_T(dsk[:, 0:16], ssm_d, 16)
        load_T(sgs[:, 0:2], subln, 2)
        ts(sgs[:], sgs[:], 1.0 - LAM_INIT, None, ALU.mult, ALU.bypass, [bConst], [bConst])
        for i in range(16):
            act(Dg[:, i, :], ident[:], AF.Copy, [bConst], [bConst], scale=dsk[:, i:i + 1])
        if do_sample:
            for q in range(2):
                load_T(Scar[:, 1 + q, 0, :], st_re[q], 64)
                load_T(Scar[:, 1 + q, 1, :], st_im[q], 64)
                for l in range(2):
                    for t2 in range(2):
                        load_T(ccar[:, l, 1 + q, :, t2], st_conv[l, q, t2].rearrange("(k f) -> k f", f=128), FC)

        lq = TMP[:, 4096:4096 + 512].rearrange("p (a b) -> p a b", a=4)
        sp_dma(lq, lamv.rearrange("a b -> (a b)").partition_broadcast(128).rearrange("p (a b) -> p a b", a=4),
               [], [bTMP], key=bTMP, nc_ok=True)
        tt(lq[:, 0, :], lq[:, 0, :], lq[:, 1, :], ALU.mult, [bTMP], [bTMP])
        tt(lq[:, 2, :], lq[:, 2, :], lq[:, 3, :], ALU.mult, [bTMP], [bTMP])
        dve(lambda e: e.reduce_sum(out=lamt[:, 0:1], in_=lq[:, 0, :], axis=mybir.AxisListType.X), [bTMP], [bConst])
        dve(lambda e: e.reduce_sum(out=lamt[:, 1:2], in_=lq[:, 2, :], axis=mybir.AxisListType.X), [bTMP], [bConst])
        act(lamt[:, 0:2], lamt[:, 0:2], AF.Exp, [bConst], [bConst])
        tt(lamt[:, 2:3], lamt[:, 1:2], lamt[:, 0:1], ALU.subtract, [bConst], [bConst])
        ts(lamt[:, 2:3], lamt[:, 2:3], -LAM_INIT, None, ALU.add, ALU.bypass, [bConst], [bConst])

        cs = Rr[0:3, 0:2048]
        sp_dma(cs, c_all, [], [bR], key=bR)
        act(cs, cs, AF.Silu, [bR], [bR])
        for kc in range(16):
            tp(ps[0][:, 3 * kc:3 * kc + 3], cs[0:3, kc * 128:(kc + 1) * 128], 3, [bR], [PB[0]])
        act(cTb[:].rearrange("p a b -> p (a b)"), ps[0][:, 0:48], AF.Copy, [PB[0]], [bConst])
        for l in range(2):
            for b in range(48):
                wv, wb = wget("ada", (l, b))
                for oc in range(2):
                    ch = 2 * b + oc
                    for kc in range(16):
                        mm(ps[1 + l][:, 3 * ch:3 * ch + 3], wv[:, kc, oc * 128:(oc + 1) * 128], cTb[:, kc, :],
                           kc == 0, kc == 15, [wb, bConst], [PB[1 + l]])
            tt(modt[:, l, :, :], ps[1 + l][:, 0:288].rearrange("p (c s) -> p c s", s=3),
               bada[:, l, :].unsqueeze(2).broadcast_to([128, 96, 3]), ALU.add, [PB[1 + l], bConst], [bConst])
        sh1 = lambda l: modt[:, l, 0:16, :]
        sc1 = lambda l: modt[:, l, 16:32, :]
        gt1 = lambda l: modt[:, l, 32:48, :]
        sh2 = lambda l: modt[:, l, 48:64, :]
        sc2 = lambda l: modt[:, l, 64:80, :]
        gt2 = lambda l: modt[:, l, 80:96, :]
        cc = [bConst]
        ts(A1m[:], sc1(0), 1.0, None, ALU.add, ALU.bypass, cc, cc)
        tmpm = TMP[:, 5120:5120 + 48].rearrange("p (c s) -> p c s", s=3)
        for l in range(2):
            ts(G1[:, l], gt1(l), 1.0, 1.0 / ALPHA, ALU.add, ALU.mult, cc, cc)
            ts(G2[:, l], gt2(l), 1.0, 1.0 / ALPHA, ALU.add, ALU.mult, cc, cc)
            g1 = lng[:, (l * 2 + 0) * 16:(l * 2 + 0) * 16 + 16].unsqueeze(2).broadcast_to([128, 16, 3])
            b1 = lnb[:, (l * 2 + 0) * 16:(l * 2 + 0) * 16 + 16].unsqueeze(2).broadcast_to([128, 16, 3])
            ts(tmpm, sc2(l), 1.0, None, ALU.add, ALU.bypass, cc, [bTMP])
            tt(HS2[:, l], tmpm, g1, ALU.mult, [bTMP] + cc, cc)
            tt(HB2[:, l], tmpm, b1, ALU.mult, [bTMP] + cc, cc)
            tt(HB2[:, l], HB2[:, l], sh2(l), ALU.add, cc, cc)
        g2 = lng[:, 16:32].unsqueeze(2).broadcast_to([128, 16, 3])
        b2 = lnb[:, 16:32].unsqueeze(2).broadcast_to([128, 16, 3])
        ts(tmpm, sc1(1), 1.0, None, ALU.add, ALU.bypass, cc, [bTMP])
        tt(HS1n[:], tmpm, g2, ALU.mult, [bTMP] + cc, cc)
        tt(HB1n[:], tmpm, b2, ALU.mult, [bTMP] + cc, cc)
        tt(HB1n[:], HB1n[:], sh1(1), ALU.add, cc, cc)
        fw.barrier()
        ck("s_params")

        def tmpv(i):
            return TMP[:, 4096 + 64 * i:4096 + 64 * (i + 1)]

        def sincos(dst, ang, shift, tA, tB, tI):
            ts(tA, ang, 1.0 / (2 * math.pi), 64.5 + shift, ALU.mult, ALU.add, [bTMP], [bTMP])
            dve(lambda e: e.tensor_copy(out=tI, in_=tA), [bTMP], [bTMP])
            dve(lambda e: e.tensor_copy(out=tB, in_=tI), [bTMP], [bTMP])
            tt(tA, tA, tB, ALU.subtract, [bTMP], [bTMP])
            dve(lambda e: e.tensor_single_scalar(out=tB, in_=tA, scalar=0.0, op=ALU.is_lt), [bTMP], [bTMP])
            tt(tA, tA, tB, ALU.add, [bTMP], [bTMP])
            ts(tA, tA, 2 * math.pi, -math.pi, ALU.mult, ALU.add, [bTMP], [bTMP])
            ts(tA, tA, 3.14159, -3.14159, ALU.min, ALU.max, [bTMP], [bTMP])
            act(dst, tA, AF.Sin, [bTMP], [bTMP])

        def disc(lr_in, li, dtt, o_are, o_aim, o_kr, o_ki, base):
            lr, prod, mag, cs_, sn_, tA, tB = [tmpv(base + i) for i in range(7)]
            tI = tmpv(base + 7).bitcast(I32)
            nr = tmpv(base + 8)
            ts(lr, lr_in, -1e-4, None, ALU.min, ALU.bypass, [bTMP], [bTMP])
            tt(prod, lr, dtt, ALU.mult, [bTMP], [bTMP])
            act(mag, prod, AF.Exp, [bTMP], [bTMP])
            tt(prod, li, dtt, ALU.mult, [bTMP], [bTMP])
            sincos(sn_, prod, 0.0, tA, tB, tI)
            sincos(cs_, prod, 0.25, tA, tB, tI)
            tt(o_are, mag, cs_, ALU.mult, [bTMP], [bTMP, bConst])
            tt(o_aim, mag, sn_, ALU.mult, [bTMP], [bTMP, bConst])
            if o_kr is None:
                return
            ts(nr, o_are, -1.0, None, ALU.add, ALU.bypass, [bTMP], [bTMP])
            tt(tA, lr, lr, ALU.mult, [bTMP], [bTMP])
            tt(tB, li, li, ALU.mult, [bTMP], [bTMP])
            tt(tA, tA, tB, ALU.add, [bTMP], [bTMP])
            dve(lambda e: e.reciprocal(out=tA, in_=tA), [bTMP], [bTMP])
            tt(o_kr, nr, lr, ALU.mult, [bTMP], [bTMP])
            tt(tB, o_aim, li, ALU.mult, [bTMP], [bTMP])
            tt(o_kr, o_kr, tB, ALU.add, [bTMP], [bTMP])
            tt(o_kr, o_kr, tA, ALU.mult, [bTMP], [bTMP])
            tt(o_ki, o_aim, lr, ALU.mult, [bTMP], [bTMP])
            tt(tB, nr, li, ALU.mult, [bTMP], [bTMP])
            tt(o_ki, o_ki, tB, ALU.subtract, [bTMP], [bTMP])
            tt(o_ki, o_ki, tA, ALU.mult, [bTMP], [bTMP])

        lrS, liS, dtS = tmpv(20), tmpv(21), tmpv(22)
        load_T(lrS, lam_re.rearrange("(j gg) p -> j (gg p)", gg=2), 64)
        load_T(liS, lam_im.rearrange("(j gg) p -> j (gg p)", gg=2), 64)
        k = tps()
        sp_dma(stg[k - 6][0:64, 256:258], log_step.rearrange("(j gg) -> j gg", gg=2), [], [bStg[k - 6]], key=bStg[k - 6])
        dve(lambda e, k=k: e.tensor_copy(out=stg[k - 6][0:64, 0:128].rearrange("j (gg p) -> j gg p", gg=2),
                                         in_=stg[k - 6][0:64, 256:258].unsqueeze(2).broadcast_to([64, 2, 64])),
            [bStg[k - 6]], [bStg[k - 6]])
        tp(ps[k][:, 0:64], stg[k - 6][0:64, 0:128], 64, [bStg[k - 6]], [PB[k]])
        act(dtS, ps[k][:, 0:64], AF.Exp, [PB[k]], [bTMP])
        fw.barrier()
        ck("s_ada")
        disc(lrS, liS, dtS, A1s[:, 0, :], A2s[:, 0, :], None, None, 0)
        dve(lambda e: e.tensor_copy(out=A1s[:, 1, :], in_=A1s[:, 0, :]), [bConst], [bConst])
        dve(lambda e: e.tensor_copy(out=A2s[:, 1, :], in_=A2s[:, 0, :]), [bConst], [bConst])
        lrA, liA, dtA, krA, kiA, areA, aimA = [tmpv(23 + i) for i in range(7)]
        sp_dma(lrA, lam_re, [], [bTMP], key=bTMP)
        sp_dma(liA, lam_im, [], [bTMP], key=bTMP)
        sp_dma(dtA[:, 0:1], log_step.unsqueeze(1), [], [bTMP], key=bTMP)
        act(dtA[:, 1:2], dtA[:, 0:1], AF.Exp, [bTMP], [bTMP])
        dve(lambda e: e.tensor_copy(out=dtA, in_=dtA[:, 1:2].broadcast_to([128, 64])), [bTMP], [bTMP])
        disc(lrA, liA, dtA, areA, aimA, krA, kiA, 0)
        brT = Rr[:, 0:1024].rearrange("g (p c) -> g p c", c=16)
        biT = Rr[:, 1024:2048].rearrange("g (p c) -> g p c", c=16)
        oR = Rr[:, 2048:3072].rearrange("g (c p) -> g p c", p=64)
        oI = Rr[:, 3072:4096].rearrange("g (c p) -> g p c", p=64)
        t1 = Rr[:, 4096:5120].rearrange("g (p c) -> g p c", c=16)
        sp_dma(Rr[:, 0:1024], b_re, [], [bR], key=bR)
        sp_dma(Rr[:, 1024:2048], b_im, [], [bR], key=bR)
        krB = krA.unsqueeze(2).broadcast_to([128, 64, 16])
        kiB = kiA.unsqueeze(2).broadcast_to([128, 64, 16])
        tt(oR, brT, krB, ALU.mult, [bR, bTMP], [bR])
        tt(t1, biT, kiB, ALU.mult, [bR, bTMP], [bR])
        tt(oR, oR, t1, ALU.subtract, [bR], [bR])
        tt(oI, biT, krB, ALU.mult, [bR, bTMP], [bR])
        tt(t1, brT, kiB, ALU.mult, [bR, bTMP], [bR])
        tt(oI, oI, t1, ALU.add, [bR], [bR])
        bBB = B("bb_scr")
        sp_dma(bb_scr[0].rearrange("(g c) p -> g (c p)", c=16), Rr[:, 2048:3072], [bR], [bBB], key=bR)
        sp_dma(bb_scr[1].rearrange("(g c) p -> g (c p)", c=16), Rr[:, 3072:4096], [bR], [bBB], key=bR)
        fw.barrier()
        ck("s_disc")
        dve(lambda e: e.memset(BIG[:, 8192:24576], 0.0), [bBIG], [bBIG])
        for r in range(2):
            srcv = bb_scr[r].rearrange("(i P) p -> P i p", P=128)
            for q4 in range(4):
                for gg in range(2):
                    p0 = 16 * (2 * q4 + gg)
                    dstv = BwT[p0:p0 + 16, r, :, :].rearrange("p (i q) k -> p i q k", q=4)[:, :, q4, gg * 64:(gg + 1) * 64]
                    fw.dma("pool", lambda e, dstv=dstv, s_=srcv[p0:p0 + 16]: e.dma_start(out=dstv, in_=s_),
                           reads=[bBB, bBIG, fw.fence], writes=[bBIG], key=bBIG)
        bBW = B("bw_scr")
        sp_dma(bw_scr, BIG[:, 8192:24576], [bBIG], [bBW], key=bBIG)
        dve(lambda e: e.memset(BIG[:, 0:8192], 0.0), [bBIG], [bBIG])
        CTs = Rr[0:64, 4096:8192].rearrange("p (r i k) -> p r i k", r=2, i=16)
        for r, csrc in enumerate((c_re, c_im)):
            for i in range(16):
                k = tps()
                sp_dma(stg[k - 6][:, 0:64], csrc[i * 128:(i + 1) * 128, :], [], [bStg[k - 6]], key=bStg[k - 6])
                tp(ps[k][0:64, 0:128], stg[k - 6][:, 0:64], 128, [bStg[k - 6]], [PB[k]])
                act(CTs[:, r, i, :], ps[k][0:64, 0:128], AF.Copy, [PB[k]], [bR], scale=(1.0 if r == 0 else -1.0))
        for r in range(2):
            for gg in range(2):
                for qq in range(2):
                    cb = (2 * qq + gg) * 16
                    dstv = CwI[64 * gg:64 * gg + 64, r].rearrange("p (i hf q) k -> p i hf q k", hf=2, q=2)[:, :, :, qq, cb:cb + 16]
                    srcv = CTs[:, r].rearrange("p i (hf q g c) -> p i hf q g c", hf=2, q=2, g=2)[:, :, :, qq, gg, :]
                    fw.dma("pool", lambda e, dstv=dstv, srcv=srcv: e.dma_start(out=dstv, in_=srcv),
                           reads=[bR, bBIG, fw.fence], writes=[bBIG], key=bBIG)
        bCW = B("cw_scr")
        sp_dma(cw_scr, BIG[:, 0:8192], [bBIG], [bCW], key=bBIG)
        fw.barrier()
        ck("s_ssmw")
        fw.strict_default = False

        def ln_block(l, k, T, segs, producer, gate, last_layer):
            S1, S2 = 4, 5
            stat = TMP[:, 4096:6144].rearrange("p (a t) -> p a t", a=4)
            for m in range(DC):
                src, sbufs = producer(m)
                for (c0, c1, s) in segs:
                    stt(xT[:, m, c0:c1], src[:, c0:c1], gate[:, m, s:s + 1], xT[:, m, c0:c1], ALU.mult, ALU.add,
                        sbufs + [XT[m], bConst], [XT[m]])
                rb = TMP[:, 2048 + (m % 2) * 512:2048 + (m % 2) * 512 + 256].bitcast(BF16)
                rq = TMP[:, 2048 + (m % 2) * 512 + 256:2048 + (m % 2) * 512 + 512].bitcast(BF16)
                brb = B(f"rb{m % 2}")
                act(rb[:, 0:T], xT[:, m, 0:T], AF.Copy, [XT[m]], [brb])
                mm(ps[S1][:, 0:T], ones_b[:], rb[:, 0:T], m == 0, m == DC - 1, [brb, bConst], [PB[S1]])
                brq = B(f"rq{m % 2}")
                act(rq[:, 0:T], xT[:, m, 0:T], AF.Square, [XT[m]], [brq])
                mm(ps[S2][:, 0:T], ones_b[:], rq[:, 0:T], m == 0, m == DC - 1, [brq, bConst], [PB[S2]])
            bst = B("lnstat")
            mean, var, nmr = stat[:, 0, 0:T], stat[:, 1, 0:T], stat[:, 2, 0:T]
            ts(mean, ps[S1][:, 0:T], 1.0 / D, None, ALU.mult, ALU.bypass, [PB[S1]], [bst], True)
            tt(nmr, mean, mean, ALU.mult, [bst], [bst], True)
            stt(var, ps[S2][:, 0:T], 1.0 / D, nmr, ALU.mult, ALU.subtract, [PB[S2], bst], [bst], True)
            ts(var, var, EPS_LN, None, ALU.add, ALU.bypass, [bst], [bst], True)
            act(var, var, AF.Sqrt, [bst], [bst], strict=True)
            dve(lambda e: e.reciprocal(out=var, in_=var), [bst], [bst], True)
            rstd_p, nmr_p = ps[S2][:, 0:T], ps[S1][:, 0:T]
            dve(lambda e: e.tensor_copy(out=rstd_p, in_=var), [bst], [PB[S2]], True)
            stt(nmr_p, mean, -1.0, var, ALU.mult, ALU.mult, [bst], [PB[S1]], True)
            gi = (l * 2 + k) * 16
            for m in range(DC):
                tt(xT[:, m, 0:T], xT[:, m, 0:T], rstd_p, ALU.mult, [XT[m], PB[S2]], [XT[m]], m == 0)
                tt(xT[:, m, 0:T], xT[:, m, 0:T], nmr_p, ALU.add, [XT[m], PB[S1]], [XT[m]])
                if not (last_layer and k == 1):
                    for (c0, c1, s) in segs:
                        if k == 0:
                            hs, hb = HS2[:, l, m, s:s + 1], HB2[:, l, m, s:s + 1]
                        else:
                            hs, hb = HS1n[:, m, s:s + 1], HB1n[:, m, s:s + 1]
                        act(hB[:, m, c0:c1], xT[:, m, c0:c1], AF.Identity, [XT[m], bConst], [HBb[m]], bias=hb, scale=hs)
                act(xT[:, m, 0:T], xT[:, m, 0:T], AF.Identity, [XT[m], bConst], [XT[m]],
                    bias=lnb[:, gi + m:gi + m + 1], scale=lng[:, gi + m:gi + m + 1])

        def ffn(l, T, segs, last_layer):
            gb = [TMP[:, 0:520], TMP[:, 520:1040]]
            cb = [TMP[:, 1040:1552], TMP[:, 1552:2064]]
            geb = [TMP[:, 2064:2576], TMP[:, 2576:3088]]
            ACTb = [B(f"act{f}") for f in range(FC)]
            for f in range(FC):
                wv, wb = wget("up", (l, f))
                pg, pv = (f % 2) * 2, (f % 2) * 2 + 1
                for kc in range(16):
                    mm(ps[pg][:, 0:T], wv[:, kc, 0, :], hB[:, kc, 0:T], kc == 0, kc == 15, [wb, HBb[kc]], [PB[pg]])
                for kc in range(16):
                    mm(ps[pv][:, 0:T], wv[:, kc, 1, :], hB[:, kc, 0:T], kc == 0, kc == 15, [wb, HBb[kc]], [PB[pv]])
                g_, c_, ge_ = gb[f % 2], cb[f % 2], geb[f % 2]
                bg, bc, bge = B(f"gb{f % 2}"), B(f"cb{f % 2}"), B(f"geb{f % 2}")
                for si, (c0, c1, s) in enumerate(segs):
                    n = c1 - c0
                    o = c0 + 2 * si
                    act(g_[:, o + 2:o + 2 + n], ps[pg][:, c0:c1], AF.Copy, [PB[pg]], [bg])
                    dve(lambda e, o=o, s=s, f=f, g_=g_: e.tensor_copy(out=g_[:, o:o + 2], in_=ccar[:, l, s, f, :]),
                        [B("ccar")], [bg], True)
                    dve(lambda e, o=o, n=n, s=s, f=f, g_=g_: e.tensor_copy(out=ccar[:, l, s, f, :], in_=g_[:, o + n:o + n + 2]),
                        [bg], [B("ccar")], True)
                    wi = l * 132 + f
                    ts(c_[:, c0:c1], g_[:, o + 2:o + 2 + n], wdc[:, wi + 88:wi + 89], bdc[:, l * FC + f:l * FC + f + 1],
                       ALU.mult, ALU.add, [bg, bConst], [bc])
                    stt(c_[:, c0:c1], g_[:, o + 1:o + 1 + n], wdc[:, wi + 44:wi + 45], c_[:, c0:c1], ALU.mult, ALU.add,
                        [bg, bConst, bc], [bc], True)
                    stt(c_[:, c0:c1], g_[:, o:o + n], wdc[:, wi:wi + 1], c_[:, c0:c1], ALU.mult, ALU.add,
                        [bg, bConst, bc], [bc])
                act(ge_[:, 0:T], c_[:, 0:T], AF.Gelu, [bc], [bge])
                tt(actT[:, f, 0:T], ge_[:, 0:T], ps[pv][:, 0:T], ALU.mult, [bge, PB[pv]], [ACTb[f]])
            fw.barrier()
            ck("t_ffnup")

            def prod(m):
                wv, wb = wget("down", (l, m))
                pb = m % 4
                for kc in range(FC):
                    mm(ps[pb][:, 0:T], wv[:, kc, :], actT[:, kc, 0:T], kc == 0, kc == FC - 1, [wb, ACTb[kc]], [PB[pb]])
                return ps[pb], [PB[pb]]

            ln_block(l, 1, T, segs, prod, G2[:, l], last_layer)
            fw.barrier()
            ck("t_ffn")

        def load_x(src_rows, T):
            nchunk = max(1, T // 128)
            rows = min(T, 128)
            for tc in range(nchunk):
                k = tc % 2
                sp_dma(stg[k][0:rows, :], src_rows[tc * 128:tc * 128 + rows, :], [], [bStg[k]], key=bStg[k])
                for g4 in range(4):
                    pb = tps()
                    for q in range(4):
                        c = 4 * g4 + q
                        tp(ps[pb][:, q * 128:q * 128 + rows], stg[k][0:rows, c * 128:(c + 1) * 128], rows,
                           [bStg[k]], [PB[pb]])
                    fw.op("act", lambda e, pb=pb, g4=g4, tc=tc: e.activation(
                        out=xT[:, 4 * g4:4 * g4 + 4, tc * 128:tc * 128 + rows],
                        in_=ps[pb][:, :].rearrange("p (q t) -> p q t", t=128)[:, :, 0:rows], func=AF.Copy),
                        reads=[PB[pb]], writes=[XT[4 * g4 + q] for q in range(4)])

        def store_rows(dst_rows, srcT, src_bufs, T, is_output=True, extra=None):
            nchunk = max(1, T // 128)
            rows = min(T, 128)
            for tc in range(nchunk):
                k = tc % 2
                for g4 in range(4):
                    pb = tps()
                    for q in range(4):
                        c = 4 * g4 + q
                        tp(ps[pb][0:rows, q * 128:(q + 1) * 128], srcT[:, c, tc * 128:tc * 128 + rows], 128,
                           [src_bufs[c]], [PB[pb]])
                    act(stg[k][0:rows, g4 * 512:(g4 + 1) * 512], ps[pb][0:rows, :], AF.Copy, [PB[pb]], [bStg[k]])
                sp_dma(dst_rows[tc * 128:tc * 128 + rows, :], stg[k][0:rows, :], [bStg[k]], [], key=bStg[k],
                       is_output=is_output)

        def ssm_layer(T, segs, blocks):
            UB = [B(f"u{c}") for c in range(DC)]
            for b in range(8):
                wv, wb = wget("in", b)
                for oc in range(2):
                    m = 2 * b + oc
                    pb = m % 4
                    for kc in range(16):
                        mm(ps[pb][:, 0:T], wv[:, kc, oc * 128:(oc + 1) * 128], hB[:, kc, 0:T], kc == 0, kc == 15,
                           [wb, HBb[kc]], [PB[pb]])
                    act(uB[:, m, 0:T], ps[pb][:, 0:T], AF.Copy, [PB[pb]], [UB[m]])
            fw.barrier(engines=("pe", "dve", "pool"))
            ck("t_u")
            sp_dma(BIG[:, 8192:24576], bw_scr, [bBW], [bBIG], key=bBIG)
            sp_dma(TMP[:, 0:4096].bitcast(BF16), cw_scr, [bCW], [bTMP], key=bTMP)
            JD = 48
            T1 = TMP[:, 4096:4224].rearrange("p (r j) -> p r j", r=2)
            U_ = TMP[:, 4224:4352].rearrange("p (r j) -> p r j", r=2)
            bHb = B("Hb")
            halves = [("dve", 0, JD, B("T1d"), B("Ud"), B("Btd"), B("Scd")),
                      ("pool", JD, 64, B("T1p"), B("Up"), B("Btp"), B("Scp"))]
            bBt2 = [halves[0][5], halves[1][5]]
            def Bproj(bi):
                t0, n, sq = blocks[bi]
                for r in range(2):
                    for jg in range(4):
                        bank = (2 * r + jg) % 4
                        for jj in range(16):
                            j = jg * 16 + jj
                            mm(ps[bank][:, jj * 32:jj * 32 + n], BwT[:, r, j, :], uB[:, j // 4, t0:t0 + n], True, True,
                               [bBIG, UB[j // 4]], [PB[bank]])
                        fw.op("act", lambda e, bank=bank, r=r, j0=jg * 16, n=n: e.activation(
                            out=Bt[:, r, j0:j0 + 16, 0:n],
                            in_=ps[bank][:, :].rearrange("p (j t) -> p j t", t=32)[:, :, 0:n], func=AF.Copy),
                            reads=[PB[bank]], writes=[bBt2[0 if jg * 16 < JD else 1]])

            def scan(bi):
                t0, n, sq = blocks[bi]
                for (eng, ja, jb, bT1, bU, bBt, bS) in halves:
                    def TT(out, in0, in1, op, reads, writes, strict=None, eng=eng):
                        fw.op(eng, lambda e: e.tensor_tensor(out=out, in0=in0, in1=in1, op=op), reads=reads, writes=writes,
                              strict=strict)
                    for t in range(n):
                        prev = Scar[:, sq, :, ja:jb] if t == 0 else Bt[:, :, ja:jb, t - 1]
                        pbuf = [bS] if t == 0 else [bBt]
                        TT(T1[:, :, ja:jb], A1s[:, :, ja:jb], prev, ALU.mult, pbuf + [bConst], [bT1], t == 0)
                        TT(U_[:, :, ja:jb], A2s[:, :, ja:jb], prev, ALU.mult, pbuf + [bConst], [bU], t == 0)
                        TT(T1[:, 0, ja:jb], T1[:, 0, ja:jb], U_[:, 1, ja:jb], ALU.subtract, [bT1, bU], [bT1])
                        TT(T1[:, 1, ja:jb], T1[:, 1, ja:jb], U_[:, 0, ja:jb], ALU.add, [bT1, bU], [bT1])
                        TT(Bt[:, :, ja:jb, t], Bt[:, :, ja:jb, t], T1[:, :, ja:jb], ALU.add, [bBt, bT1], [bBt])
                    fw.op(eng, lambda e, sq=sq, n=n, ja=ja, jb=jb: e.tensor_copy(out=Scar[:, sq, :, ja:jb],
                                                                                 in_=Bt[:, :, ja:jb, n - 1]),
                          reads=[bBt], writes=[bS], strict=True)

            def cast(bi):
                t0, n, sq = blocks[bi]
                for r in range(2):
                    act(HbT[:, r, :, 0:n], Bt[:, r, :, 0:n], AF.Copy, bBt2, [bHb])

            def Cproj(bi):
                t0, n, sq = blocks[bi]
                bank = 4 + (bi % 2)
                for i in range(16):
                    oc = ps[bank][:, i * 32:i * 32 + n]
                    mm(oc, Dg[:, i, :], uB[:, i, t0:t0 + n], True, False, [bConst, UB[i]], [PB[bank]])
                    for hf in range(2):
                        for qq in range(2):
                            j = 4 * i + 2 * hf + qq
                            for r in range(2):
                                lastmm = (hf == 1 and qq == 1 and r == 1)
                                mm(ps[bank][64 * hf:64 * hf + 64, i * 32:i * 32 + n], CwT[:, r, j, :],
                                   HbT[:, r, j, 0:n], False, lastmm, [bTMP, bHb], [PB[bank]])
                fw.op("act", lambda e, bank=bank, t0=t0, n=n: e.activation(
                    out=hB[:, :, t0:t0 + n],
                    in_=ps[bank][:, :].rearrange("p (j t) -> p j t", t=32)[:, :, 0:n], func=AF.Gelu),
                    reads=[PB[bank]], writes=HBb)

            for bi in range(len(blocks)):
                Bproj(bi)
                scan(bi)
                if bi >= 1:
                    Cproj(bi - 1)
                cast(bi)
            Cproj(len(blocks) - 1)
            fw.barrier(engines=("pe", "dve", "pool"))
            ck("t_scan")

            def prod(m):
                wv, wb = wget("glu", m)
                pa, pg = (m % 2) * 2, (m % 2) * 2 + 1
                for kc in range(16):
                    mm(ps[pa][:, 0:T], wv[:, kc, 0, :], hB[:, kc, 0:T], kc == 0, kc == 15, [wb, HBb[kc]], [PB[pa]])
                for kc in range(16):
                    mm(ps[pg][:, 0:T], wv[:, kc, 1, :], hB[:, kc, 0:T], kc == 0, kc == 15, [wb, HBb[kc]], [PB[pg]])
                sg = TMP[:, (m % 2) * 512:(m % 2) * 512 + 512]
                mx = TMP[:, 1024 + (m % 2) * 512:1024 + (m % 2) * 512 + 512]
                bsg, bmx = B(f"sg{m % 2}"), B(f"mx{m % 2}")
                act(sg[:, 0:T], ps[pg][:, 0:T], AF.Sigmoid, [PB[pg]], [bsg])
                tt(mx[:, 0:T], ps[pa][:, 0:T], sg[:, 0:T], ALU.mult, [PB[pa], bsg], [bmx])
                return mx, [bmx]

            ln_block(0, 0, T, segs, prod, G1[:, 0], False)
            fw.barrier()
            ck("t_ssm")

        def attn_full(kind, ti, T, segs):
            QT = [B(f"q{c}") for c in range(DC)]
            bKf = B("kTf")
            for b in range(16):
                wv, wb = wget("qkv", b)
                for oc in range(2):
                    m = (2 * b + oc) % 16
                    pb = (2 * b + oc) % 4
                    for kc in range(16):
                        mm(ps[pb][:, 0:T], wv[:, kc, oc * 128:(oc + 1) * 128], hB[:, kc, 0:T], kc == 0, kc == 15,
                           [wb, HBb[kc]], [PB[pb]])
                    if b < 8:
                        act(qT[:, m, 0:T], ps[pb][:, 0:T], AF.Copy, [PB[pb]], [QT[m]], scale=128.0 ** -0.5)
                    else:
                        act(kTf[:, m, 0:T], ps[pb][:, 0:T], AF.Copy, [PB[pb]], [bKf])
            kdst = k_p[ti * TT:(ti + 1) * TT, :] if kind == "p" else k_s
            store_rows(kdst, kTf, [bKf] * 16, T)
            fw.barrier()
            ck("t_qk")
            kTb16 = TMP[:, 4096:6144].bitcast(BF16).rearrange("p (c t) -> p c t", t=256)
            bk16 = B("kTb16")
            knew = TMP[:, 4096:5120].bitcast(BF16).rearrange("p (c t) -> p c t", t=128)
            if kind == "p":
                for hh in range(2):
                    for c in range(16):
                        act(kTb16[:, c, :], kTf[:, c, hh * 256:(hh + 1) * 256], AF.Copy, [bKf], [bk16])
                    sp_dma(kT_scr[:, :, ti * TT + hh * 256:ti * TT + (hh + 1) * 256].rearrange("s d t -> d s t"),
                           kTb16, [bk16], [bKscr], key=bk16)
            else:
                for c in range(16):
                    act(knew[:, c, 0:64], kTf[:, c, 0:64], AF.Copy, [bKf], [bk16])
            vnew = Rr[0:32, 4096:8192].bitcast(BF16).rearrange("p (q e) -> p q e", q=2)
            bvn = B("vnew")
            vs32 = [TMP[:, 0:256], TMP[:, 256:512], TMP[:, 512:768], TMP[:, 768:1024]]
            vs16 = [TMP[:, 1024:1152].bitcast(BF16), TMP[:, 1152:1280].bitcast(BF16),
                    TMP[:, 1280:1408].bitcast(BF16), TMP[:, 1408:1536].bitcast(BF16)]
            if kind == "p":
                units = [(tc, 128, tc * 128) for tc in range(T // 128)]
            else:
                units = [(q, 32, q * 32) for q in range(2)]
            cnt = 0
            for b in range(8):
                wv, wb = wget("qkv", 16 + b)
                for (ui, rows, c0) in units:
                    pb = cnt % 4
                    sidx = cnt % 4
                    cnt += 1
                    for kc in range(16):
                        mm(ps[pb][0:rows, 0:256], hB[:, kc, c0:c0 + rows], wv[:, kc, :], kc == 0, kc == 15,
                           [wb, HBb[kc]], [PB[pb]])
                    b32, b16 = B(f"vs32_{sidx}"), B(f"vs16_{sidx}")
                    act(vs32[sidx][0:rows, :], ps[pb][0:rows, 0:256], AF.Copy, [PB[pb]], [b32])
                    if kind == "p":
                        act(vs16[sidx][0:rows, :], ps[pb][0:rows, 0:256], AF.Copy, [PB[pb]], [b16])
                        r0 = ti * TT + c0
                        sp_dma(v_p[r0:r0 + rows, b * 256:(b + 1) * 256], vs32[sidx][0:rows, :], [b32], [], key=b32,
                               is_output=True)
                        sp_dma(v_scr[r0:r0 + rows, b * 256:(b + 1) * 256], vs16[sidx][0:rows, :], [b16], [bVscr], key=b16)
                    else:
                        act(vnew[0:rows, ui, b * 256:(b + 1) * 256], ps[pb][0:rows, 0:256], AF.Copy, [PB[pb]], [bvn])
                        sp_dma(v_s[c0:c0 + rows, b * 256:(b + 1) * 256], vs32[sidx][0:rows, :], [b32], [], key=b32,
                               is_output=True)
            fw.barrier()
            ck("t_v")

            Pt = [TMP[:, 0:256].bitcast(BF16), TMP[:, 256:512].bitcast(BF16), TMP[:, 512:768].bitcast(BF16)]
            bPt = [B("Pt0"), B("Pt1"), B("Pt2")]
            On = [TMP[:, 768:1792].rearrange("p (a t) -> p a t", a=2), TMP[:, 1792:2816].rearrange("p (a t) -> p a t", a=2)]
            bOn = [B("On0"), B("On1")]
            rec = TMP[:, 2816:3328]
            brec = B("rec")
            osq = TMP[:, 3328:3840].bitcast(BF16).rearrange("p (a t) -> p a t", a=2)
            bosq = B("osq")
            rr = TMP[:, 2816:3328]
            brr = B("rr")
            kst = TMP[:, 5120:6144].rearrange("p (k e) -> p k e", e=256)
            bkst = B("kst")
            KT1 = Rr[:, 0:4096].bitcast(BF16).rearrange("p (s t) -> p s t", s=2)
            Vh1 = Rr[:, 4096:8192].bitcast(BF16).rearrange("p (k e) -> p k e", e=256)
            KTs, Vhs = [KTt, KT1], [Vht, Vh1]
            bKTs, bVhs = [B("KTt"), B("KT1")], [B("Vht"), B("Vh1")]
            if kind == "p":
                qsets = [(0, T, None)]
            else:
                qsets = [(0, 32, 0), (32, 64, 1)]
            pend = []
            for (q0, q1, sq) in qsets:
                nq = q1 - q0
                for h in range(NH):
                    bs_ = (h % 2) if kind == "p" else 0
                    KTc, Vhc, bKT, bVh = KTs[bs_], Vhs[bs_], bKTs[bs_], bVhs[bs_]
                    chunks = []
                    if kind == "p":
                        kend = (ti + 1) * TT
                        sp_dma(KTc[:, :, 0:kend], kT_scr[2 * h:2 * h + 2, :, 0:kend].rearrange("s d t -> d s t"),
                               [bKscr], [bKT], key=bKT)
                        sp_dma(Vhc[:, 0:kend // 128, :],
                               v_scr[0:kend, h * 256:(h + 1) * 256].rearrange("(k p) e -> p k e", p=128),
                               [bVscr], [bVh], key=bVh)
                        for kc in range(kend // 128):
                            dg = kc - 4 * ti
                            c_lo = 0 if dg < 0 else 128 * dg
                            chunks.append((kc * 128, 128, (lambda ec, kc=kc, Vhc=Vhc: Vhc[:, kc, ec * 128:(ec + 1) * 128]), c_lo,
                                           dg >= 0, [bVh]))
                    else:
                        kstK = TMP[:, 5120:5632].rearrange("p (k e) -> p k e", e=256)
                        kstV = TMP[:, 5632:6144].rearrange("p (k e) -> p k e", e=256)
                        bkK, bkV = B("kstK"), B("kstV")
                        for g8 in range(8):
                            sp_dma(kstK, cache_k[sq, g8 * 256:(g8 + 1) * 256, 2 * h * 128:(2 * h + 2) * 128]
                                   .rearrange("(k p) e -> p k e", p=128), [], [bkK], key=bkK)
                            sp_dma(kstV, cache_v[sq, g8 * 256:(g8 + 1) * 256, h * 256:(h + 1) * 256]
                                   .rearrange("(k p) e -> p k e", p=128), [], [bkV], key=bkV)
                            for s2 in range(2):
                                pb = tps()
                                for k2 in range(2):
                                    tp(ps[pb][:, k2 * 128:(k2 + 1) * 128], kstK[:, k2, s2 * 128:(s2 + 1) * 128], 128,
                                       [bkK], [PB[pb]])
                                act(KTc[:, s2, g8 * 256:(g8 + 1) * 256], ps[pb][:, 0:256], AF.Copy, [PB[pb]], [bKT])
                            act(Vhc[:, g8 * 2:(g8 + 1) * 2, :], kstV, AF.Copy, [bkV], [bVh])
                        for s2 in range(2):
                            act(KTc[:, s2, PAST:PAST + 32], knew[:, 2 * h + s2, q0:q1], AF.Copy, [bk16], [bKT])
                        for kc in range(16):
                            chunks.append((kc * 128, 128, (lambda ec, kc=kc, Vhc=Vhc: Vhc[:, kc, ec * 128:(ec + 1) * 128]), 0,
                                           False, [bVh]))
                        chunks.append((PAST, 32, (lambda ec, sq=sq: vnew[0:32, sq, h * 256 + ec * 128:h * 256 + (ec + 1) * 128]),
                                       0, False, [bvn]))
                    for s2 in range(2):
                        sub = 2 * h + s2
                        nck = len(chunks)
                        ab = 2 + 3 * s2

                        def score(ci):
                            koff, nk, vsrc, c_lo, diag, vb = chunks[ci]
                            sb_ = ci % 2
                            pt, bpt = Pt[ci % 3], bPt[ci % 3]
                            mm(ps[sb_][0:nk, c_lo:nq], KTc[:, s2, koff:koff + nk], qT[:, sub, q0 + c_lo:q1], True, True,
                               [bKT, QT[sub]], [PB[sb_]])
                            act(pt[0:nk, c_lo:nq], ps[sb_][0:nk, c_lo:nq], AF.Exp, [PB[sb_]], [bpt])
                            if diag:
                                dve(lambda e, pt=pt, c_lo=c_lo: e.memset(pt[64:128, c_lo:c_lo + 64], 0.0), [bpt], [bpt])

                        def pv(ci):
                            koff, nk, vsrc, c_lo, diag, vb = chunks[ci]
                            pt, bpt = Pt[ci % 3], bPt[ci % 3]
                            for ec in range(2):
                                mm(ps[ab + ec][:, c_lo:nq], vsrc(ec), pt[0:nk, c_lo:nq], ci == 0, ci == nck - 1,
                                   vb + [bpt], [PB[ab + ec]])
                            mm(ps[ab + 2][:, c_lo:nq], ones_b[0:nk, :], pt[0:nk, c_lo:nq], ci == 0, ci == nck - 1,
                               [bConst, bpt], [PB[ab + 2]])

                        for ci in range(nck + 1):
                            if ci < nck:
                                score(ci)
                            if ci >= 1:
                                pv(ci - 1)
                        if s2 == 0 and pend:
                            pend.pop()()
                        dve(lambda e, nq=nq, ab=ab: e.reciprocal(out=rec[:, 0:nq], in_=ps[ab + 2][:, 0:nq]), [PB[ab + 2]],
                            [brec], True)
                        for ec in range(2):
                            tt(On[s2][:, ec, 0:nq], ps[ab + ec][:, 0:nq], rec[:, 0:nq], ALU.mult, [PB[ab + ec], brec],
                               [bOn[s2]], True)
                    for ec in range(2):
                        stt(On[0][:, ec, 0:nq], On[1][:, ec, 0:nq], lamt[:, 2:3], On[0][:, ec, 0:nq], ALU.mult, ALU.add,
                            [bOn[1], bOn[0], bConst], [bOn[0]], True)
                        tt(osq[:, ec, 0:nq], On[0][:, ec, 0:nq], On[0][:, ec, 0:nq], ALU.mult, [bOn[0]], [bosq], True)
                    def epilogue(h=h, q0=q0, q1=q1, nq=nq):
                        for ec in range(2):
                            mm(ps[7][:, 0:nq], ones_b[:], osq[:, ec, 0:nq], ec == 0, ec == 1, [bosq, bConst], [PB[7]])
                        ts(rr[:, 0:nq], ps[7][:, 0:nq], 1.0 / 256.0, LN_EPS, ALU.mult, ALU.add, [PB[7], brec], [brr, brec], True)
                        act(rr[:, 0:nq], rr[:, 0:nq], AF.Ln, [brr], [brr])
                        act(rr[:, 0:nq], rr[:, 0:nq], AF.Exp, [brr], [brr], scale=-0.5, strict=True)
                        for ec in range(2):
                            stt(hB[:, 2 * h + ec, q0:q1], On[0][:, ec, 0:nq], sgs[:, ec:ec + 1], rr[:, 0:nq], ALU.mult,
                                ALU.mult, [bOn[0], brr, bConst], [HBb[2 * h + ec]], True)

                    pend.append(epilogue)
            if pend:
                pend.pop()()
            fw.barrier()
            ck("t_attn")

            wcache = {}

            def prod(m):
                b = m // 2
                if b not in wcache:
                    wcache.clear()
                    wcache[b] = wget("o", b)
                wv, wb = wcache[b]
                pb = m % 4
                for kc in range(16):
                    mm(ps[pb][:, 0:T], wv[:, kc, (m % 2) * 128:(m % 2) * 128 + 128], hB[:, kc, 0:T], kc == 0, kc == 15,
                       [wb, HBb[kc]], [PB[pb]])
                return ps[pb], [PB[pb]]

            ln_block(1, 0, T, segs, prod, G1[:, 1], True)
            fw.barrier()
            ck("t_wo")

        for (kind, ti) in tiles:
            if kind == "p":
                T = TT
                segs = [(0, TT, 0)]
                blocks = [(32 * b, 32, 0) for b in range(TT // 32)]
                src_rows = x_p[ti * TT:(ti + 1) * TT, :]
                dst_rows = y_p[ti * TT:(ti + 1) * TT, :]
            else:
                T = 64
                segs = [(0, 32, 1), (32, 64, 2)]
                blocks = [(0, 32, 1), (32, 32, 2)]
                src_rows = x_s
                dst_rows = y_s
            load_x(src_rows, T)
            for m in range(DC):
                for (c0, c1, s) in segs:
                    act(hB[:, m, c0:c1], xT[:, m, c0:c1], AF.Identity, [XT[m], bConst], [HBb[m]],
                        bias=modt[:, 0, m, s:s + 1], scale=A1m[:, m, s:s + 1])
            fw.barrier()
            ck("t_load")
            def dbg(i):
                if debug:
                    store_rows(dbg_p[i, ti * TT:(ti + 1) * TT, :] if kind == "p" else dbg_s[i], xT, XT, T)
                    fw.barrier()
                    if i == 0:
                        for c in range(16):
                            act(kTf[:, c, 0:T], hB[:, c, 0:T], AF.Copy, [HBb[c]], [bR])
                        store_rows(dbg_p[3, ti * TT:(ti + 1) * TT, :] if kind == "p" else dbg_s[3], kTf, [bR] * 16, T)
                        fw.barrier()
            ssm_layer(T, segs, blocks)
            dbg(0)
            ffn(0, T, segs, False)
            dbg(1)
            attn_full(kind, ti, T, segs)
            dbg(2)
            ffn(1, T, segs, True)
            store_rows(dst_rows, xT, XT, T)
            fw.barrier()
            ck("t_tile")

        store_T(sre_p, Scar[:, 0, 0, :], 64, [B("Scar")])
        store_T(sim_p, Scar[:, 0, 1, :], 64, [B("Scar")])
        for l in range(2):
            for t2 in range(2):
                store_T(conv_p[l, t2].rearrange("(k f) -> k f", f=128), ccar[:, l, 0, :, t2], FC, [bConst])
        if do_sample:
            for q in range(2):
                store_T(sre_s[q], Scar[:, 1 + q, 0, :], 64, [B("Scar")])
                store_T(sim_s[q], Scar[:, 1 + q, 1, :], 64, [B("Scar")])
                for l in range(2):
                    for t2 in range(2):
                        store_T(conv_s[l, q, t2].rearrange("(k f) -> k f", f=128), ccar[:, l, 1 + q, :, t2], FC, [bConst])
        fw.finish()
        fw.emit()
    return nc


def make_in_maps(inp, n_cores=8, n_tiles=8):
    f = lambda a: np.ascontiguousarray(np.asarray(a, dtype=np.float32))
    SEQ = TT * n_tiles
    shared = {
        "w_ada": f(inp["w_ada"]), "b_ada": f(inp["b_ada"]).reshape(2, 96, 128),
        "ln_g": f(inp["ln_g"]).reshape(64, 128), "ln_b": f(inp["ln_b"]).reshape(64, 128),
        "w_up": f(inp["w_up"]), "w_dconv": f(inp["w_dconv"]).reshape(264, 128),
        "b_dconv": f(inp["b_dconv"]).reshape(88, 128), "w_down": f(inp["w_down"]),
        "w_in": f(inp["w_ssm_in"][0]), "lam_re": f(inp["ssm_lam_re"][0]), "lam_im": f(inp["ssm_lam_im"][0]),
        "log_step": f(inp["ssm_log_step"][0]), "b_re": f(inp["ssm_b_re"][0]).reshape(128, 1024),
        "b_im": f(inp["ssm_b_im"][0]).reshape(128, 1024), "c_re": f(inp["ssm_c_re"][0]).reshape(2048, 64),
        "c_im": f(inp["ssm_c_im"][0]).reshape(2048, 64), "ssm_d": f(inp["ssm_d"][0]).reshape(16, 128),
        "w_glu": f(inp["w_glu"][0]), "w_qkv": f(inp["w_qkv"][0]),
        "lamv": f(np.stack([inp["lam_q1"][0], inp["lam_k1"][0], inp["lam_q2"][0], inp["lam_k2"][0]])),
        "subln": f(inp["subln_g"][0]).reshape(2, 128), "w_o": f(inp["w_o"][0]),
    }
    maps = []
    for c in range(n_cores):
        m = dict(shared)
        m["x_p"] = f(inp["x_prompt"][c, :SEQ])
        m["x_s"] = f(inp["x_sample"][2 * c:2 * c + 2]).reshape(64, D)
        m["c_all"] = f(np.concatenate([inp["c_prompt"][c:c + 1], inp["c_sample"][2 * c:2 * c + 2]], axis=0))
        m["cache_k"] = f(inp["cache_k"][0, 2 * c:2 * c + 2]).reshape(2, PAST, D)
        m["cache_v"] = f(inp["cache_v"][0, 2 * c:2 * c + 2]).reshape(2, PAST, D)
        m["st_re"] = f(inp["state_ssm_re"][0, 2 * c:2 * c + 2]).reshape(2, 64, 128)
        m["st_im"] = f(inp["state_ssm_im"][0, 2 * c:2 * c + 2]).reshape(2, 64, 128)
        m["st_conv"] = f(np.asarray(inp["state_conv"])[:, 2 * c:2 * c + 2])
        maps.append(m)
    return maps


def assemble(results, n_cores=8, n_tiles=8):
    SEQ = TT * n_tiles
    g = lambda name: [np.asarray(r[name], dtype=np.float32) for r in results]
    y_p = np.stack(g("y_p")).reshape(n_cores, SEQ, D)
    y_s = np.concatenate([a.reshape(2, 32, D) for a in g("y_s")], axis=0)
    k_p = np.stack(g("k_p")).reshape(1, n_cores, SEQ, 16, 128)
    v_p = np.stack(g("v_p")).reshape(1, n_cores, SEQ, 8, 256)
    sre_p = np.stack(g("sre_p")).reshape(1, n_cores, 128, 64)
    sim_p = np.stack(g("sim_p")).reshape(1, n_cores, 128, 64)
    conv_p = np.stack(g("conv_p"), axis=1).reshape(2, n_cores, 2, DFF)
    k_s = np.concatenate([a.reshape(2, 32, 16, 128) for a in g("k_s")], axis=0)[None]
    v_s = np.concatenate([a.reshape(2, 32, 8, 256) for a in g("v_s")], axis=0)[None]
    sre_s = np.concatenate([a.reshape(2, 128, 64) for a in g("sre_s")], axis=0)[None]
    sim_s = np.concatenate([a.reshape(2, 128, 64) for a in g("sim_s")], axis=0)[None]
    conv_s = np.concatenate(g("conv_s"), axis=1)
    return (y_p, y_s, k_p, v_p, sre_p, sim_p, conv_p, k_s, v_s, sre_s, sim_s, conv_s)


_NC_CACHE = {}


def kernel(**inputs):
    if "nc" not in _NC_CACHE:
        _NC_CACHE["nc"] = build()
    nc = _NC_CACHE["nc"]
    maps = make_in_maps(inputs)
    res = run_bass_kernel_spmd(nc, maps, core_ids=list(range(8)))
    return assemble(res.results)
```
